# Optimizing a Trainium2 kernel written in Bass

```python
import math
import jax, jax.numpy as jnp
from jax import lax
import numpy as np

D_MODEL = 1024
BATCH = 16
SEQ = 256
DEPTH = 2
DEC_BATCH = 8
DEC_SEQ = 4096
PAST_LEN = 512

GRID_W = 64
MIX_WIDTH = D_MODEL
GROUP_WIDTH = MIX_WIDTH // 4
MLA_HEADS = 4
MLA_NOPE = 64
MLA_ROPE = 32
MLA_V = 64
MLA_Q_RANK = 192
MLA_KV_RANK = 128
LRU_WIDTH = GROUP_WIDTH
LRU_BLOCKS = 4
LRU_CONV = 4
LRU_C = 8.0
POOL_WINDOWS = (2, 4, 8, 16)
POOL_CH = GROUP_WIDTH // len(POOL_WINDOWS)
DIFF_HEADS = 4
DIFF_DIM = GROUP_WIDTH // (2 * DIFF_HEADS)
FF_HIDDEN = -(-8 * D_MODEL // (3 * 256)) * 256
ROPE_BASE = 10000.0
Q_BLOCK = 128
EPS = 1e-6
MLA_IN = MLA_Q_RANK + MLA_KV_RANK + MLA_ROPE
LRU_IN = 2 * LRU_WIDTH
POOL_IN = GROUP_WIDTH
DIFF_QK = DIFF_HEADS * 2 * DIFF_DIM
DIFF_IN = 2 * DIFF_QK + DIFF_HEADS * 2 * DIFF_DIM
IN_COLS = MLA_IN + LRU_IN + POOL_IN + DIFF_IN

kernel_name = 'hybrid_diffusion_prefix_step'


def _rms(x, g):
    xf = x.astype(jnp.float32)
    y = xf * lax.rsqrt(jnp.mean(xf * xf, axis=-1, keepdims=True) + EPS)
    return (y * g.astype(jnp.float32)).astype(x.dtype)


def _axial_rope(n, rot_dim):
    rows = n // GRID_W
    row = jnp.repeat(jnp.arange(rows), GRID_W).astype(jnp.float32)
    col = jnp.tile(jnp.arange(GRID_W), rows).astype(jnp.float32)
    quarter = rot_dim // 4
    inv = ROPE_BASE ** (-jnp.arange(quarter, dtype=jnp.float32) / quarter)
    ang = jnp.concatenate([row[:, None] * inv, col[:, None] * inv], axis=-1)
    return jnp.cos(ang), jnp.sin(ang)


def _rope(x, cs):
    cos, sin = cs
    half = x.shape[-1] // 2
    shape = (1, x.shape[1]) + (1,) * (x.ndim - 3) + (half,)
    c = cos.reshape(shape).astype(x.dtype)
    s = sin.reshape(shape).astype(x.dtype)
    x1, x2 = x[..., :half], x[..., half:]
    return jnp.concatenate([x1 * c - x2 * s, x1 * s + x2 * c], axis=-1)


def _blockwise(q, fn):
    B, N = q.shape[:2]
    blk = min(Q_BLOCK, N)
    nb = N // blk
    qb = jnp.moveaxis(q.reshape((B, nb, blk) + q.shape[2:]), 1, 0)
    out = lax.map(fn, qb)
    out = jnp.moveaxis(out, 0, 1)
    return out.reshape((B, N) + out.shape[3:])


def _softmax_attention(q, k, v, scale):
    def blk(qb):
        s = jnp.einsum('bqhe,bkhe->bhqk', qb, k).astype(jnp.float32) * scale
        p = jax.nn.softmax(s, axis=-1).astype(v.dtype)
        return jnp.einsum('bhqk,bkhf->bqhf', p, v)
    return _blockwise(q, blk)


def _mla(u, lp, rope_cs, ctx):
    B, N, _ = u.shape
    c_q = u[..., :MLA_Q_RANK]
    c_kv = u[..., MLA_Q_RANK:MLA_Q_RANK + MLA_KV_RANK]
    k_r = u[..., MLA_Q_RANK + MLA_KV_RANK:]
    q = (_rms(c_q, lp['mla_q_norm_g']) @ lp['mla_w_uq']).reshape(B, N, MLA_HEADS, MLA_NOPE + MLA_ROPE)
    q_nope, q_rope = q[..., :MLA_NOPE], q[..., MLA_NOPE:]
    kv_lat = _rms(c_kv, lp['mla_kv_norm_g'])
    if ctx is None:
        lat_all, kr_all = kv_lat, k_r
    else:
        q_rope = _rope(q_rope, rope_cs)
        lat_all = jnp.concatenate([ctx[0], kv_lat], axis=1)
        kr_all = jnp.concatenate([ctx[1], _rope(k_r, rope_cs)], axis=1)
    K = lat_all.shape[1]
    kv = (lat_all @ lp['mla_w_ukv']).reshape(B, K, MLA_HEADS, MLA_NOPE + MLA_V)
    k = jnp.concatenate([kv[..., :MLA_NOPE],
                         jnp.broadcast_to(kr_all[:, :, None, :], (B, K, MLA_HEADS, MLA_ROPE))], axis=-1)
    v = kv[..., MLA_NOPE:]
    qf = jnp.concatenate([q_nope, q_rope], axis=-1)
    o = _softmax_attention(qf, k, v, 1.0 / math.sqrt(MLA_NOPE + MLA_ROPE))
    return o.reshape(B, N, MLA_HEADS * MLA_V), (kv_lat, k_r)


def _conv_centred(x, w, b):
    N = x.shape[1]
    xp = jnp.pad(x, ((0, 0), (1, LRU_CONV - 2), (0, 0)))
    y = b
    for j in range(LRU_CONV):
        y = y + xp[:, j:j + N] * w[j]
    return y


def _lin_combine(e1, e2):
    a1, b1 = e1
    a2, b2 = e2
    return a1 * a2, a2 * b1 + b2


def _rglru(u, lp, h0):
    B, N, _ = u.shape
    xb, gb = u[..., :LRU_WIDTH], u[..., LRU_WIDTH:]
    xc = _conv_centred(xb, lp['lru_conv_w'], lp['lru_conv_b'])
    xg = xc.reshape(B, N, LRU_BLOCKS, LRU_WIDTH // LRU_BLOCKS)

    def gate(w, bias):
        z = jnp.einsum('bngc,zgce->zbnge', xg, w).reshape(2, B, N, LRU_WIDTH) + bias[:, None, None, :]
        return jax.nn.sigmoid(z.astype(jnp.float32))

    r = gate(lp['lru_w_r'], lp['lru_b_r'])
    i = gate(lp['lru_w_i'], lp['lru_b_i'])
    log_a = -LRU_C * r * jax.nn.softplus(-lp['lru_lambda'].astype(jnp.float32))[:, None, None, :]
    a = jnp.exp(log_a)
    bt = jnp.sqrt(1.0 - jnp.exp(2.0 * log_a)) * i * xc.astype(jnp.float32)[None]
    a = jnp.stack([a[0], jnp.flip(a[1], axis=1)])
    bt = jnp.stack([bt[0], jnp.flip(bt[1], axis=1)])
    if h0 is not None:
        bt = bt.at[:, :, 0].add(a[:, :, 0] * jnp.moveaxis(h0, 1, 0).astype(jnp.float32))
    _, h = lax.associative_scan(_lin_combine, (a, bt), axis=2)
    y = (h[0] + jnp.flip(h[1], axis=1)).astype(u.dtype) * jax.nn.gelu(gb)
    final = jnp.moveaxis(h[:, :, -1], 0, 1).astype(u.dtype) if h0 is None else None
    return y, final


def _pool(u, lp):
    B, N, _ = u.shape
    uf = u.astype(jnp.float32)
    cs = jnp.concatenate([jnp.zeros((B, 1, POOL_IN), jnp.float32), jnp.cumsum(uf, axis=1)], axis=1)
    t = jnp.arange(N)
    outs = []
    for g, w in enumerate(POOL_WINDOWS):
        lo = jnp.clip(t - w // 2, 0, N)
        hi = jnp.clip(t + w // 2, 0, N)
        seg = cs[:, :, g * POOL_CH:(g + 1) * POOL_CH]
        mean = (seg[:, hi] - seg[:, lo]) / (hi - lo).astype(jnp.float32)[None, :, None]
        outs.append(mean - uf[..., g * POOL_CH:(g + 1) * POOL_CH])
    d = jnp.stack(outs, axis=2).astype(u.dtype)
    y = jnp.einsum('bngc,gce->bnge', d, lp['pool_w']).reshape(B, N, POOL_IN)
    return y * lp['pool_scale']


def _diff(u, lp, layer_idx, rope_cs, ctx):
    B, N, _ = u.shape
    q = u[..., :DIFF_QK].reshape(B, N, DIFF_HEADS, 2, DIFF_DIM)
    k = u[..., DIFF_QK:2 * DIFF_QK].reshape(B, N, DIFF_HEADS, 2, DIFF_DIM)
    v = u[..., 2 * DIFF_QK:].reshape(B, N, DIFF_HEADS, 2 * DIFF_DIM)
    if ctx is None:
        k_all, v_all = k, v
    else:
        q = _rope(q, rope_cs)
        k_all = jnp.concatenate([ctx[0], _rope(k, rope_cs)], axis=1)
        v_all = jnp.concatenate([ctx[1], v], axis=1)
    lam_init = 0.8 - 0.6 * math.exp(-0.3 * layer_idx)
    lv = lp['diff_lambda'].astype(jnp.float32)
    lam = jnp.exp(jnp.sum(lv[0] * lv[1])) - jnp.exp(jnp.sum(lv[2] * lv[3])) + lam_init
    scale = 1.0 / math.sqrt(DIFF_DIM)

    def blk(qb):
        s = jnp.einsum('bqhce,bkhce->bhcqk', qb, k_all).astype(jnp.float32) * scale
        p = jax.nn.softmax(s, axis=-1)
        att = (p[:, :, 0] - lam * p[:, :, 1]).astype(v_all.dtype)
        return jnp.einsum('bhqk,bkhf->bqhf', att, v_all)

    o = _blockwise(q, blk)
    o = _rms(o, lp['diff_norm_g']) * (1.0 - lam_init)
    return o.reshape(B, N, DIFF_HEADS * 2 * DIFF_DIM), (k, v)


def _layer(x, cond, lp, layer_idx, rope_mla, rope_diff, ctx):
    mod = (jax.nn.silu(cond) @ lp['w_ada'] + lp['b_ada'])[:, None, :]
    sh1, sc1, g1, sh2, sc2, g2 = jnp.split(mod, 6, axis=-1)
    h = _rms(x, lp['norm1_g']) * (1.0 + sc1) + sh1
    u = h @ lp['w_in']
    o1 = MLA_IN
    o2 = o1 + LRU_IN
    o3 = o2 + POOL_IN
    y_mla, (ckv, kr) = _mla(u[..., :o1], lp, rope_mla, None if ctx is None else ctx[0:2])
    y_lru, st = _rglru(u[..., o1:o2], lp, None if ctx is None else ctx[4])
    y_pool = _pool(u[..., o2:o3], lp)
    y_diff, (dk, dv) = _diff(u[..., o3:], lp, layer_idx, rope_diff, None if ctx is None else ctx[2:4])
    mix = jnp.concatenate([y_mla, y_lru, y_pool, y_diff], axis=-1) @ lp['w_out']
    x = x + g1 * mix
    h = _rms(x, lp['norm2_g']) * (1.0 + sc2) + sh2
    gu = h @ lp['w_gu']
    x = x + g2 * ((jax.nn.silu(gu[..., :FF_HIDDEN]) * gu[..., FF_HIDDEN:]) @ lp['w_down'])
    return x, (ckv, kr, dk, dv, st)


def setup_inputs(seed: int = 0) -> dict:
    key = jax.random.key(seed)
    ks = list(jax.random.split(key, 40))

    def nrm(idx, shape, s=1.0):
        return jax.random.normal(ks[idx], shape, jnp.float32) * s

    a_c = jax.random.uniform(ks[30], (DEPTH, 2, LRU_WIDTH), jnp.float32, minval=0.9, maxval=0.999)
    a0 = a_c ** (1.0 / LRU_C)
    lru_lambda = jnp.log(a0) - jnp.log1p(-a0)
    return {
        'x_prompt': nrm(0, (BATCH, SEQ, D_MODEL)),
        'x_sample': nrm(1, (DEC_BATCH, DEC_SEQ, D_MODEL)),
        'cache_mla_ckv': nrm(2, (DEC_BATCH, DEPTH, PAST_LEN, MLA_KV_RANK)),
        'cache_mla_krope': nrm(3, (DEC_BATCH, DEPTH, PAST_LEN, MLA_ROPE)),
        'cache_diff_k': nrm(4, (DEC_BATCH, DEPTH, PAST_LEN, DIFF_HEADS, 2, DIFF_DIM)),
        'cache_diff_v': nrm(5, (DEC_BATCH, DEPTH, PAST_LEN, DIFF_HEADS, 2 * DIFF_DIM)),
        'state_lru': nrm(6, (DEC_BATCH, DEPTH, 2, LRU_WIDTH), 0.5),
        'c': nrm(7, (DEC_BATCH, D_MODEL)),
        'c_ctx': nrm(8, (D_MODEL,)),
        'w_ada': nrm(9, (DEPTH, D_MODEL, 6 * D_MODEL), 0.5 * D_MODEL ** -0.5),
        'b_ada': nrm(10, (DEPTH, 6 * D_MODEL), 0.01),
        'norm1_g': 1.0 + nrm(11, (DEPTH, D_MODEL), 0.01),
        'norm2_g': 1.0 + nrm(12, (DEPTH, D_MODEL), 0.01),
        'w_in': nrm(13, (DEPTH, D_MODEL, IN_COLS), D_MODEL ** -0.5),
        'mla_q_norm_g': 1.0 + nrm(14, (DEPTH, MLA_Q_RANK), 0.01),
        'mla_w_uq': nrm(15, (DEPTH, MLA_Q_RANK, MLA_HEADS * (MLA_NOPE + MLA_ROPE)), MLA_Q_RANK ** -0.5),
        'mla_kv_norm_g': 1.0 + nrm(16, (DEPTH, MLA_KV_RANK), 0.01),
        'mla_w_ukv': nrm(17, (DEPTH, MLA_KV_RANK, MLA_HEADS * (MLA_NOPE + MLA_V)), MLA_KV_RANK ** -0.5),
        'lru_conv_w': nrm(18, (DEPTH, LRU_CONV, LRU_WIDTH), LRU_CONV ** -0.5),
        'lru_conv_b': nrm(19, (DEPTH, LRU_WIDTH), 0.01),
        'lru_w_r': nrm(20, (DEPTH, 2, LRU_BLOCKS, LRU_WIDTH // LRU_BLOCKS, LRU_WIDTH // LRU_BLOCKS), (LRU_WIDTH // LRU_BLOCKS) ** -0.5),
        'lru_b_r': nrm(21, (DEPTH, 2, LRU_WIDTH), 0.01),
        'lru_w_i': nrm(22, (DEPTH, 2, LRU_BLOCKS, LRU_WIDTH // LRU_BLOCKS, LRU_WIDTH // LRU_BLOCKS), (LRU_WIDTH // LRU_BLOCKS) ** -0.5),
        'lru_b_i': nrm(23, (DEPTH, 2, LRU_WIDTH), 0.01),
        'lru_lambda': lru_lambda,
        'pool_w': nrm(24, (DEPTH, len(POOL_WINDOWS), POOL_CH, POOL_CH), POOL_CH ** -0.5),
        'pool_scale': 1.0 + nrm(25, (DEPTH, POOL_IN), 0.1),
        'diff_lambda': nrm(26, (DEPTH, 4, DIFF_DIM), 0.1),
        'diff_norm_g': 1.0 + nrm(27, (DEPTH, 2 * DIFF_DIM), 0.01),
        'w_out': nrm(28, (DEPTH, MIX_WIDTH, D_MODEL), MIX_WIDTH ** -0.5),
        'w_gu': nrm(29, (DEPTH, D_MODEL, 2 * FF_HIDDEN), D_MODEL ** -0.5),
        'w_down': nrm(31, (DEPTH, FF_HIDDEN, D_MODEL), FF_HIDDEN ** -0.5),
        'final_norm_g': 1.0 + nrm(32, (D_MODEL,), 0.01),
    }


def reference(x_prompt, x_sample, cache_mla_ckv, cache_mla_krope, cache_diff_k, cache_diff_v, state_lru,
              c, c_ctx, w_ada, b_ada, norm1_g, norm2_g, w_in, mla_q_norm_g, mla_w_uq, mla_kv_norm_g,
              mla_w_ukv, lru_conv_w, lru_conv_b, lru_w_r, lru_b_r, lru_w_i, lru_b_i, lru_lambda, pool_w,
              pool_scale, diff_lambda, diff_norm_g, w_out, w_gu, w_down, final_norm_g):
    stacked = {
        'w_ada': w_ada, 'b_ada': b_ada, 'norm1_g': norm1_g, 'norm2_g': norm2_g, 'w_in': w_in,
        'mla_q_norm_g': mla_q_norm_g, 'mla_w_uq': mla_w_uq, 'mla_kv_norm_g': mla_kv_norm_g,
        'mla_w_ukv': mla_w_ukv, 'lru_conv_w': lru_conv_w, 'lru_conv_b': lru_conv_b,
        'lru_w_r': lru_w_r, 'lru_b_r': lru_b_r, 'lru_w_i': lru_w_i, 'lru_b_i': lru_b_i,
        'lru_lambda': lru_lambda, 'pool_w': pool_w, 'pool_scale': pool_scale,
        'diff_lambda': diff_lambda, 'diff_norm_g': diff_norm_g, 'w_out': w_out,
        'w_gu': w_gu, 'w_down': w_down,
    }
    n_lat = x_sample.shape[1]
    rope_mla = _axial_rope(n_lat, MLA_ROPE)
    rope_diff = _axial_rope(n_lat, DIFF_DIM)
    cond_ctx = c_ctx[None, :]
    xp, xs = x_prompt, x_sample
    ckv_l, kr_l, dk_l, dv_l, st_l = [], [], [], [], []
    for l in range(DEPTH):
        lp = {name: arr[l] for name, arr in stacked.items()}
        xp, (ckv, kr, dk, dv, st) = _layer(xp, cond_ctx, lp, l, None, None, None)
        ckv_l.append(ckv)
        kr_l.append(kr)
        dk_l.append(dk)
        dv_l.append(dv)
        st_l.append(st)
        ctx = (cache_mla_ckv[:, l], cache_mla_krope[:, l], cache_diff_k[:, l], cache_diff_v[:, l], state_lru[:, l])
        xs, _ = _layer(xs, c, lp, l, rope_mla, rope_diff, ctx)
    y_prompt = _rms(xp, final_norm_g)
    y_sample = _rms(xs, final_norm_g)
    new_mla_ckv = jnp.stack(ckv_l, axis=1)
    new_mla_krope = jnp.stack(kr_l, axis=1)
    new_diff_k = jnp.stack(dk_l, axis=1)
    new_diff_v = jnp.stack(dv_l, axis=1)
    new_state_lru = jnp.stack(st_l, axis=1)
    return (y_prompt, y_sample, new_mla_ckv, new_mla_krope, new_diff_k, new_diff_v, new_state_lru)
```

```python
import math
from contextlib import ExitStack

import numpy as np
import concourse.bass as bass
import concourse.mybir as mybir
from concourse.ap import AP
from concourse.bass_utils import run_bass_kernel_spmd

F32 = mybir.dt.float32
BF = mybir.dt.bfloat16
ALU = mybir.AluOpType
AF = mybir.ActivationFunctionType
AX = mybir.AxisListType

D = 1024
DEPTH = 2
NS = 4096
NP = 256
PAST = 512
GRID_W = 64
IN_COLS = 1888
FF = 2816
EPS = 1e-6
LRU_C = 8.0
O_CQ, O_CKV, O_KR, O_XB, O_GB, O_PU, O_DQ, O_DK, O_DV = 0, 192, 320, 352, 608, 864, 1120, 1376, 1632


class Buf:
    __slots__ = ("w", "r", "base", "name", "psum")

    def __init__(self, name="", psum=False):
        self.psum = psum
        self.w = {}
        self.base = {}
        self.r = {}
        self.name = name


class T:
    __slots__ = ("ap", "buf")

    def __init__(self, ap, buf):
        self.ap = ap
        self.buf = buf

    def __getitem__(self, k):
        return T(self.ap[k], self.buf)

    def re(self, s, **kw):
        return T(self.ap.rearrange(s, **kw), self.buf)

    def bitcast(self, dt):
        return T(self.ap.bitcast(dt), self.buf)


def _ap(x):
    return x.ap if isinstance(x, T) else x


class Sched:
    def __init__(self, nc, n_dma=20, needed=None):
        self.needed = needed
        self.rec = set()
        self.idx = {k: 0 for k in ("pe", "act", "dve", "pool")}
        self.nc = nc
        self.E = {"pe": nc.tensor, "act": nc.scalar, "dve": nc.vector, "pool": nc.gpsimd, "sp": nc.sync}
        self.sem = {k: nc.alloc_semaphore("s_" + k) for k in ("pe", "act", "dve", "pool")}
        self.cnt = {k: 0 for k in self.sem}
        self.waited = {}
        self.dq = {}
        for q in ("sp", "pool", "act"):
            self.dq[q] = dict(sems=[nc.alloc_semaphore(f"d_{q}{i}") for i in range(n_dma)], cnt=[0] * n_dma, rr=0)
        self.n_ins = 0

    def _wait(self, eng, tok):
        sem, val = tok
        key = (eng, sem.name)
        if self.waited.get(key, 0) >= val:
            return
        self.E[eng].wait_ge(sem, val)
        self.waited[key] = val
        self.n_ins += 1
        if self.needed is None:
            self.rec.add((sem.name, val))

    def _maybe(self, eng, tok):
        if eng == "pe" and tok[0] is self.sem["pe"]:
            return
        self._wait(eng, tok)

    def _deps(self, eng, reads, writes, join):
        own = self.sem.get(eng)
        for b in reads:
            for tok in list(b.w.values()):
                self._maybe(eng, tok)
            if b.psum:
                for tok in list(b.r.values()):
                    if tok[0] is not own:
                        self._maybe(eng, tok)
        for b in writes:
            for tok in list((b.base if join else b.w).values()):
                self._maybe(eng, tok)
            for tok in list(b.r.values()):
                self._maybe(eng, tok)

    def _record(self, tok, reads, writes, join):
        name = tok[0].name
        for b in reads:
            b.r[name] = tok
        for b in writes:
            if join:
                b.w[name] = tok
            else:
                b.w = {name: tok}
                b.base = {name: tok}
            b.r = {}

    def op(self, eng, fn, reads=(), writes=(), join=False):
        reads = [b for b in reads if b is not None]
        self._deps(eng, reads, writes, join)
        ins = fn(self.E[eng])
        self.idx[eng] += 1
        if self.needed is None or (self.sem[eng].name, self.idx[eng]) in self.needed:
            self.cnt[eng] += 1
            ins.then_inc(self.sem[eng], 1)
        self.n_ins += 1
        self._record((self.sem[eng], self.cnt[eng]), reads, writes, join)

    def dma(self, q, out, in_, join=False, **kw):
        d = self.dq[q]
        i = d["rr"]
        d["rr"] = (i + 1) % len(d["sems"])
        sem = d["sems"][i]
        if d["cnt"][i] > 0:
            self._wait(q, (sem, d["cnt"][i]))
        reads, writes = [in_.buf], [out.buf]
        self._deps(q, reads, writes, join)
        ins = self.E[q].dma_start(out=out.ap, in_=in_.ap, **kw)
        d["cnt"][i] += 16
        ins.then_inc(sem, 16)
        self.n_ins += 1
        self._record((sem, d["cnt"][i]), reads, writes, join)

    def all_tokens(self):
        toks = [(self.sem[k], self.cnt[k]) for k in self.sem if self.cnt[k] > 0]
        for d in self.dq.values():
            toks += [(s, c) for s, c in zip(d["sems"], d["cnt"]) if c > 0]
        return toks

    def barrier(self, engines=("pe", "act", "dve", "pool", "sp")):
        toks = self.all_tokens()
        for e in engines:
            for tok in toks:
                if e in self.sem and tok[0] is self.sem[e]:
                    continue
                self._wait(e, tok)

    def finish(self):
        for tok in self.all_tokens():
            self._wait("sp", tok)


class MK:
    def __init__(self, debug_stop=None, dbg=False, needed=None):
        self.nc = nc = bass.Bass("TRN2", target_bir_lowering=False)
        self.S = Sched(nc, needed=needed)
        self.debug_stop = debug_stop
        self.dbg = dbg
        self.cast_rr = 0
        self.declare_io()

    def dram(self, name, shape, dt, kind="Internal"):
        if kind == "Internal" and self.dbg:
            kind = "ExternalOutput"
        return T(self.nc.dram_tensor(name, list(shape), dt, kind=kind).ap(), Buf(name))

    def uname(self, name):
        self.uid = getattr(self, "uid", 0) + 1
        return f"{name}_u{self.uid}"

    def sb(self, es, name, shape, dt):
        h = es.enter_context(self.nc.sbuf_tensor(self.uname(name), list(shape), dt))
        return T(h.ap(), Buf(name))

    def ps(self, es, name, shape=(128, 512), dt=F32):
        h = es.enter_context(self.nc.psum_tensor(self.uname(name), list(shape), dt))
        return T(h.ap(), Buf(name, psum=True))

    @staticmethod
    def _ce(eng):
        return eng

    def _rw(self, outs, ins):
        return [x.buf for x in ins if isinstance(x, T)], [x.buf for x in outs]

    def mm(self, out, lhsT, rhs, start=True, stop=True):
        r, w = self._rw([out], [lhsT, rhs])
        self.S.op("pe", lambda e: e.matmul(out.ap, lhsT.ap, rhs.ap, start=start, stop=stop), r, w)

    def tr(self, out, in_, ident):
        r, w = self._rw([out], [in_, ident])
        self.S.op("pe", lambda e: e.transpose(out.ap, in_.ap, ident.ap), r, w)

    def act(self, out, in_, func, bias=0.0, scale=1.0, accum=None, join=False):
        r, w = self._rw([out] + ([accum] if accum is not None else []), [in_, bias, scale])
        kw = {}
        if accum is not None:
            kw["accum_out"] = accum.ap
        self.S.op("act", lambda e: e.activation(out.ap, in_.ap, func, bias=_ap(bias), scale=_ap(scale), **kw), r, w, join)

    def tt(self, eng, out, a, b, op, join=False):
        eng = self._ce(eng)
        r, w = self._rw([out], [a, b])
        self.S.op(eng, lambda e: e.tensor_tensor(out.ap, a.ap, b.ap, op), r, w, join)

    def ts(self, eng, out, a, s1, op0, s2=None, op1=None, join=False):
        eng = self._ce(eng)
        r, w = self._rw([out], [a, s1, s2])
        if op1 is None:
            self.S.op(eng, lambda e: e.tensor_scalar(out.ap, a.ap, _ap(s1), None, op0), r, w, join)
        else:
            self.S.op(eng, lambda e: e.tensor_scalar(out.ap, a.ap, _ap(s1), _ap(s2), op0, op1), r, w, join)

    def stt(self, eng, out, a, s, b, op0, op1, join=False):
        eng = "dve"
        r, w = self._rw([out], [a, s, b])
        self.S.op(eng, lambda e: e.scalar_tensor_tensor(out.ap, a.ap, _ap(s), b.ap, op0, op1), r, w, join)

    def cp(self, eng, out, in_, join=False):
        eng = self._ce(eng)
        if eng == "act":
            return self.act(out, in_, AF.Copy, join=join)
        r, w = self._rw([out], [in_])
        self.S.op(eng, lambda e: e.tensor_copy(out.ap, in_.ap), r, w, join)

    def recip(self, out, in_, join=False):
        r, w = self._rw([out], [in_])
        self.S.op("dve", lambda e: e.reciprocal(out.ap, in_.ap), r, w, join)

    def memset(self, eng, out, val, join=False):
        eng = self._ce(eng)
        self.S.op(eng, lambda e: e.memset(out.ap, val), [], [out.buf], join)

    def scan(self, out, a, b, init):
        r, w = self._rw([out], [a, b, init])
        self.S.op("dve", lambda e: e.tensor_tensor_scan(out.ap, a.ap, b.ap, _ap(init), ALU.mult, ALU.add), r, w)

    def dma(self, out, in_, q="sp", join=False, **kw):
        self.S.dma(q, out, in_, join=join, **kw)

    def cast_eng(self):
        e = ("dve", "pool", "act")[self.cast_rr % 3]
        self.cast_rr += 1
        return e

    def declare_io(self):
        I = lambda n, s, dt=F32: self.dram(n, s, dt, "ExternalInput")
        O = lambda n, s: self.dram(n, s, F32, "ExternalOutput")
        self.xs = I("xs", [NS, D])
        self.xp = I("xp", [2 * NP, D])
        self.ckv = I("ckv", [DEPTH, PAST, 128])
        self.ckr = I("ckr", [DEPTH, PAST, 32])
        self.cdk = I("cdk", [DEPTH, PAST, 256])
        self.cdv = I("cdv", [DEPTH, PAST, 256])
        self.cst = I("cst", [DEPTH, 2, 256])
        self.cb = I("cb", [1, D])
        self.cctx = I("cctx", [1, D])
        self.w_ada = I("w_ada", [DEPTH, D, 6 * D])
        self.b_ada = I("b_ada", [DEPTH, 6 * D])
        self.norm1_g = I("norm1_g", [DEPTH, D])
        self.norm2_g = I("norm2_g", [DEPTH, D])
        self.w_in = I("w_in", [DEPTH, D, IN_COLS])
        self.mla_q_norm_g = I("mla_q_norm_g", [DEPTH, 192])
        self.mla_w_uq = I("mla_w_uq", [DEPTH, 192, 384])
        self.mla_kv_norm_g = I("mla_kv_norm_g", [DEPTH, 128])
        self.mla_w_ukv = I("mla_w_ukv", [DEPTH, 128, 512])
        self.lru_conv_w = I("lru_conv_w", [DEPTH, 4, 256])
        self.lru_conv_b = I("lru_conv_b", [DEPTH, 256])
        self.lru_w_r = I("lru_w_r", [DEPTH, 2, 4, 64, 64])
        self.lru_b_r = I("lru_b_r", [DEPTH, 2, 256])
        self.lru_w_i = I("lru_w_i", [DEPTH, 2, 4, 64, 64])
        self.lru_b_i = I("lru_b_i", [DEPTH, 2, 256])
        self.lru_lambda = I("lru_lambda", [DEPTH, 2, 256])
        self.pool_w = I("pool_w", [DEPTH, 4, 64, 64])
        self.pool_scale = I("pool_scale", [DEPTH, 256])
        self.diff_lambda = I("diff_lambda", [DEPTH, 128])
        self.diff_norm_g = I("diff_norm_g", [DEPTH, 64])
        self.w_out = I("w_out", [DEPTH, D, D])
        self.w_gu = I("w_gu", [DEPTH, D, 2 * FF])
        self.w_down = I("w_down", [DEPTH, FF, D])
        self.final_g = I("final_norm_g", [1, D])
        self.c_ident = I("c_ident", [128, 128])
        self.c_ropeC = I("c_ropeC", [128, NS])
        self.c_ropeS = I("c_ropeS", [128, NS])
        self.c_pinvS = I("c_pinvS", [4, NS])
        self.c_pinvP = I("c_pinvP", [4, NP])
        self.ys = O("ys", [NS, D])
        self.yp = O("yp", [2 * NP, D])
        self.o_ckv = O("o_ckv", [2, DEPTH, NP, 128])
        self.o_kr = O("o_kr", [2, DEPTH, NP, 32])
        self.o_dk = O("o_dk", [2, DEPTH, NP, 256])
        self.o_dv = O("o_dv", [2, DEPTH, NP, 256])
        self.o_st = O("o_st", [2, DEPTH, 2, 256])
        self.WinB = self.dram("WinB", [D, IN_COLS], BF)
        self.WoB = self.dram("WoB", [D, D], BF)
        self.WguB = self.dram("WguB", [D, 2 * FF], BF)
        self.WdB = self.dram("WdB", [FF, D], BF)
        self.seqs = []
        for name, n, nk in (("S", NS, NS + PAST), ("P0", NP, NP), ("P1", NP, NP)):
            q = dict(name=name, n=n, nk=nk, koff=nk - n, prompt=(name != "S"))
            q["pi"] = {"S": -1, "P0": 0, "P1": 1}[name]
            q["QTm"] = self.dram("QTm" + name, [4, 96, n], BF)
            q["KTm"] = self.dram("KTm" + name, [4, 96, nk], BF)
            q["Vm"] = self.dram("Vm" + name, [nk, 256], BF)
            q["DQ"] = self.dram("DQ" + name, [256, n], BF)
            q["DK"] = self.dram("DK" + name, [256, nk], BF)
            q["DV"] = self.dram("DV" + name, [nk, 256], BF)
            q["LXB"] = self.dram("LXB" + name, [256, n], F32)
            q["LGB"] = self.dram("LGB" + name, [256, n], F32)
            q["PU"] = self.dram("PU" + name, [256, n], F32)
            q["YT"] = self.dram("YT" + name, [D, n], BF)
            q["XA"] = self.dram("XA" + name, [n, D], F32)
            q["XB"] = self.dram("XB" + name, [n, D], F32)
            q["XC"] = self.dram("XC" + name, [n, D], F32)
            if name == "S":
                q["xin"], q["yout"] = self.xs, self.ys
            else:
                k = q["pi"]
                q["xin"] = T(self.xp.ap[k * NP:(k + 1) * NP, :], Buf())
                q["yout"] = T(self.yp.ap[k * NP:(k + 1) * NP, :], Buf())
            self.seqs.append(q)

    def load_cast(self, dst, src, rows, cols, stage, kcs=None):
        KC = rows // 128
        srcv = src.re("(k p) n -> p k n", p=128)
        maxe = stage[0].ap.shape[1]
        cw = min(cols, 512)
        kper = max(1, min(KC, maxe // cw))
        i = 0
        for c0 in range(0, cols, cw):
            c1 = min(cols, c0 + cw)
            for k0 in range(0, KC, kper):
                k1 = min(KC, k0 + kper)
                st = stage[i % len(stage)]
                i += 1
                n = (k1 - k0) * (c1 - c0)
                sv = T(st.ap[:, 0:n].rearrange("p (k n) -> p k n", k=k1 - k0), st.buf)
                self.dma(sv, srcv[:, k0:k1, c0:c1])
                self.cp(self.cast_eng(), dst[:, k0:k1, c0:c1], sv, join=True)

    def load_bf16(self, dst, src, rows, cols):
        KC = rows // 128
        srcv = src.re("(k p) n -> p k n", p=128)
        kper = max(1, min(KC, 4096 // cols))
        for i, k0 in enumerate(range(0, KC, kper)):
            k1 = min(KC, k0 + kper)
            self.dma(dst[:, k0:k1, :], srcv[:, k0:k1, :], join=(i > 0))

    def precast_jobs(self, L):
        jobs = []
        lst = [(self.w_out[L], self.WoB, D, D), (self.w_gu[L], self.WguB, D, 2 * FF), (self.w_down[L], self.WdB, FF, D)]
        if L + 1 < getattr(self, "nlayers", DEPTH):
            lst.append((self.w_in[L + 1], self.WinB, D, IN_COLS))
        for src, dst, rows, cols in lst:
            sv = src.re("(k p) n -> p k n", p=128)
            dv = dst.re("(k p) n -> p k n", p=128)
            KC = rows // 128
            for c0 in range(0, cols, 512):
                c1 = min(cols, c0 + 512)
                for k0 in range(0, KC, 4):
                    k1 = min(KC, k0 + 4)
                    jobs.append((sv[:, k0:k1, c0:c1], dv[:, k0:k1, c0:c1], k1 - k0, c1 - c0))
        self.pc_jobs = jobs
        self.pc_i = 0

    def precast_step(self, n=1):
        def views(job, i):
            src, dst, nk, nc_ = job
            f, b = self.pc_f[i % 2], self.pc_b[i % 2]
            fv = T(f.ap[:, 0:nk * nc_].rearrange("p (k n) -> p k n", k=nk), f.buf)
            bv = T(b.ap[:, 0:nk * nc_].rearrange("p (k n) -> p k n", k=nk), b.buf)
            return fv, bv

        for _ in range(n):
            cur = getattr(self, "pc_cur", None)
            if cur is None and not self.pc_jobs:
                return
            if self.pc_jobs:
                nxt = (self.pc_jobs.pop(0), self.pc_i)
                self.pc_i += 1
                fv, _ = views(*nxt)
                self.dma(fv, nxt[0][0], q="pool")
            else:
                nxt = None
            if cur is not None:
                fv, bv = views(*cur)
                self.cp("pool", bv, fv)
                self.dma(cur[0][1], bv, q="pool")
            self.pc_cur = nxt

    def load_cols(self, dst, src1d, n, q="sp"):
        self.dma(dst, T(src1d.ap.rearrange("(c p) -> p c", p=128), src1d.buf), q=q, allow_slow_non_contiguous=True)

    def build(self):
        nc = self.nc
        with ExitStack() as g:
            self.ident_f = self.sb(g, "ident_f", [128, 128], F32)
            self.ident_b = self.sb(g, "ident_b", [128, 128], BF)
            self.ones_b = self.sb(g, "ones_b", [128, 128], BF)
            self.ones_f = self.sb(g, "ones_f", [1, 128], F32)
            self.dma(self.ident_f, self.c_ident)
            self.cp("dve", self.ident_b, self.ident_f)
            self.memset("dve", self.ones_b, 1.0)
            self.memset("dve", self.ones_f, 1.0)
            self.modD = [[self.dram(f"modD{L}_{c}", [1, 6 * D], F32) for c in range(2)] for L in range(DEPTH)]
            self.fin_g = self.sb(g, "fin_g", [128, D], F32)
            self.dma(self.fin_g, T(self.final_g.ap[0:1, :].partition_broadcast(128), self.final_g.buf))
            for L in range(getattr(self, 'nlayers', DEPTH)):
                if L == 0:
                    self.phase0_mod(L)
                    self.S.barrier()
                self.phase1_inproj(L)
                self.S.barrier()
                if self.debug_stop == ("p1", L):
                    break
                with ExitStack() as lay:
                    self.pc_f = [self.sb(lay, f"pcf{i}", [128, 2048], F32) for i in range(2)]
                    self.pc_b = [self.sb(lay, f"pcb{i}", [128, 2048], BF) for i in range(2)]
                    self.Va = self.sb(lay, "Va", [128, 36, 4, 128], BF)
                    self.memset("pool", self.Va, 1.0)
                    self.precast_jobs(L)
                    self.phase2_mla(L)
                    self.S.barrier()
                    self.phase3_diff(L)
                    self.precast_step(10 ** 6)
                    self.S.barrier()
                self.phase4_lru(L)
                self.S.barrier()
                self.phase6_outproj(L)
                self.S.barrier()
                self.phase7_ffn(L)
                self.S.barrier()
            self.S.finish()
        return nc

    def phase0_mod(self, L):
        with ExitStack() as es:
            for _ in self.p0_gen(L, es):
                pass

    def p0_gen(self, L, es):
        CW = 256
        ccol = self.sb(es, "ccol", [128, 2, 8], F32)
        sil = self.sb(es, "sil", [128, 2, 8], BF)
        brow = [self.sb(es, f"brow{i}", [1, CW], F32) for i in range(2)]
        grow = [self.sb(es, f"grow{i}", [1, CW], F32) for i in range(2)]
        orow = [self.sb(es, f"orow{i}", [1, 2, CW], F32) for i in range(2)]
        wf = [self.sb(es, f"p0wf{i}", [128, 8, CW], F32) for i in range(2)]
        wb = [self.sb(es, f"p0wb{i}", [128, 8, CW], BF) for i in range(2)]
        pp = [self.ps(es, f"p0p{i}") for i in range(2)]
        self.dma(ccol[:, 0, :], T(self.cb.ap.rearrange("o (c p) -> p (o c)", p=128), self.cb.buf), allow_slow_non_contiguous=True)
        self.dma(ccol[:, 1, :], T(self.cctx.ap.rearrange("o (c p) -> p (o c)", p=128), self.cctx.buf), join=True, allow_slow_non_contiguous=True)
        self.act(sil, ccol, AF.Silu)
        wv = self.w_ada[L].re("(k p) n -> p k n", p=128)
        nch = 6 * D // CW
        gsrc = {1: self.norm1_g, 4: self.norm2_g}

        def load(j):
            self.dma(wf[j % 2], wv[:, :, j * CW:(j + 1) * CW])
            self.dma(brow[j % 2], self.b_ada[L:L + 1, j * CW:(j + 1) * CW])
            seg = (j * CW) // D
            if seg in gsrc:
                c0 = j * CW - seg * D
                self.dma(grow[j % 2], gsrc[seg][L:L + 1, c0:c0 + CW])

        load(0)
        yield
        for j in range(nch):
            if j + 1 < nch:
                load(j + 1)
            seg = (j * CW) // D
            self.cp("dve", wb[j % 2], wf[j % 2])
            p = pp[j % 2]
            o = orow[j % 2]
            for ci in range(2):
                for k in range(8):
                    self.mm(p[0:1, ci * CW:(ci + 1) * CW], sil[:, ci, k:k + 1], wb[j % 2][:, k, :], start=(k == 0), stop=(k == 7))
            for ci in range(2):
                self.tt("dve", o[:, ci, :], p[0:1, ci * CW:(ci + 1) * CW], brow[j % 2], ALU.add, join=(ci > 0))
                if seg in gsrc:
                    self.stt("dve", o[:, ci, :], o[:, ci, :], 1.0, grow[j % 2], ALU.add, ALU.mult)
            for ci in range(2):
                self.dma(self.modD[L][ci][0:1, j * CW:(j + 1) * CW], o[:, ci, :], q="sp", join=True)
            yield

    def mod_rows(self, es, L, segs, pref):
        out = {}
        for ci in range(2):
            for sg_ in segs:
                t = self.sb(es, f"{pref}m{ci}_{sg_}", [128, D], F32)
                src = self.modD[L][ci]
                self.dma(t, T(src.ap[0:1, sg_ * D:(sg_ + 1) * D].partition_broadcast(128), src.buf))
                out[(ci, sg_)] = t
        return out

    def norm_batch(self, xs, A, B, outs, ss, sd):
        n = len(xs)
        for i in range(n):
            self.act(outs[i], xs[i], AF.Square, accum=ss[:, i:i + 1], join=(i > 0))
        self.act(sd[:, 0:n], ss[:, 0:n], AF.Sqrt, bias=self.eps_col, scale=1.0 / D)
        self.recip(sd[:, 0:n], sd[:, 0:n])
        for i in range(n):
            self.stt("dve", xs[i], xs[i], sd[:, i:i + 1], A, ALU.mult, ALU.mult)
        for i in range(n):
            self.tt("dve", outs[i], xs[i], B, ALU.add)

    def norm_rows(self, es_tmp, xt, A, B, out, tmp, junk, ss, sd, eng2="pool"):
        self.act(junk, xt, AF.Square, accum=ss)
        self.act(sd, ss, AF.Sqrt, bias=self.eps_col, scale=1.0 / D)
        self.recip(sd, sd)
        if B is None:
            self.stt("dve", out, xt, sd, A, ALU.mult, ALU.mult)
        else:
            self.stt("dve", tmp, xt, sd, A, ALU.mult, ALU.mult)
            self.tt(eng2, out, tmp, B, ALU.add)

    def phase1_inproj(self, L):
        with ExitStack() as es:
            sb = lambda n, s, dt=F32: self.sb(es, n, s, dt)
            stage = [sb(f"stg{i}", [128, 2048]) for i in range(2)]
            Win = sb("Win", [128, 8, IN_COLS], BF)
            WinP = sb("WinP", [128, 8, 544], BF)
            Wq = sb("Wq", [128, 2, 4, 96], BF)
            WqP = sb("WqP", [128, 2, 4, 96], BF)
            Wk = sb("Wk", [128, 4, 96], BF)
            Wv = sb("Wv", [128, 4, 64], BF)
            Isel = sb("Isel", [32, 96], BF)
            gq = sb("gq", [128, 2])
            gkv = sb("gkv", [128, 1])
            self.eps_col = sb("eps_col", [128, 1])
            self.memset("dve", self.eps_col, EPS)
            mr = self.mod_rows(es, L, (0, 1), "p1")
            if L == 0:
                self.load_cast(Win, self.w_in[L], D, IN_COLS, stage)
            else:
                self.load_bf16(Win, self.WinB, D, IN_COLS)
            for (dst0, src0, nb) in ((0, O_DQ, 16), (512, O_KR, 1)):
                src = Win[:, :, src0:src0 + nb * 32].re("p k (b h j) -> p k b h j", h=2, j=16)
                dst = WinP[:, :, dst0:dst0 + nb * 32].re("p k (b h j) -> p k b h j", h=2, j=16)
                self.ts("dve", dst[:, :, :, 0, :], src[:, :, :, 1, :], -1.0, ALU.mult, join=True)
                self.cp("pool", dst[:, :, :, 1, :], src[:, :, :, 0, :], join=True)
            wq_st = sb("wq_st", [128, 2, 384])
            self.memset("pool", wq_st, 0.0)
            self.dma(wq_st[:, 0, :], self.mla_w_uq[L, 0:128, :], join=True)
            self.dma(wq_st[0:64, 1, :], self.mla_w_uq[L, 128:192, :], join=True)
            wq4 = wq_st.re("p k (h e) -> p k h e", e=96)
            self.cp("dve", Wq, wq4)
            self.memset("pool", WqP, 0.0)
            self.ts("dve", WqP[:, :, :, 64:80], wq4[:, :, :, 80:96], -1.0, ALU.mult)
            self.cp("dve", WqP[:, :, :, 80:96], wq4[:, :, :, 64:80], join=True)
            wkv_st = sb("wkv_st", [128, 512])
            self.dma(wkv_st, self.mla_w_ukv[L])
            wkv4 = wkv_st.re("p (h e) -> p h e", e=128)
            self.memset("pool", Wk, 0.0)
            self.cp("dve", Wk[:, :, 0:64], wkv4[:, :, 0:64])
            self.cp("dve", Wv, wkv4[:, :, 64:128])
            self.memset("pool", Isel, 0.0)
            self.cp("pool", Isel[:, 64:96], self.ident_b[0:32, 0:32])
            self.dma(gq[:, 0:1], T(self.mla_q_norm_g.ap[L, 0:128].rearrange("(p o) -> p o", o=1), self.mla_q_norm_g.buf))
            self.dma(gq[0:64, 1:2], T(self.mla_q_norm_g.ap[L, 128:192].rearrange("(p o) -> p o", o=1), self.mla_q_norm_g.buf), join=True)
            self.dma(gkv, T(self.mla_kv_norm_g.ap[L, :].rearrange("(p o) -> p o", o=1), self.mla_kv_norm_g.buf))
            xt = [sb(f"xt{i}", [128, D]) for i in range(4)]
            hn = [sb(f"hn{i}", [128, D], BF) for i in range(4)]
            ss4 = sb("ss4", [128, 4])
            sd4 = sb("sd4", [128, 4])
            hT = [sb(f"hT{i}", [128, 8, 512], BF) for i in range(2)]
            ropeC = [sb(f"ropeC{i}", [128, 512]) for i in range(2)]
            ropeS = [sb(f"ropeS{i}", [128, 512]) for i in range(2)]
            sq = sb("sq", [128, 2, 512], BF)
            sq2 = sb("sq2", [128, 512], BF)
            rst = sb("rst", [128, 512])
            rst2 = sb("rst2", [128, 512])
            cqn = sb("cqn", [128, 2, 512], BF)
            lat = sb("lat", [128, 512], BF)
            latf = sb("latf", [128, 512])
            krT = sb("krT", [32, 512], BF)
            krf = sb("krf", [32, 512])
            t1 = [sb(f"t1_{i}", [128, 512]) for i in range(2)]
            t2 = [sb(f"t2_{i}", [128, 512]) for i in range(2)]
            ob = [sb(f"ob{i}", [128, 512], BF) for i in range(4)]
            of = [sb(f"of{i}", [128, 512]) for i in range(4)]
            vb = [sb(f"vb{i}", [128, 256], BF) for i in range(2)]
            vf = [sb(f"vf{i}", [128, 256]) for i in range(2)]
            cst_f = sb("cst_f", [128, 4, 256])
            cst_b = sb("cst_b", [128, 4, 256], BF)
            pT2 = [self.ps(es, f"pT{i}") for i in range(2)]
            pT = pT2[0]
            pQ = [self.ps(es, f"pQ{i}") for i in range(2)]
            pC = self.ps(es, "pC")
            pR = [self.ps(es, f"pR{i}") for i in range(3)]
            rr = [0]
            obr = [0]
            tr_ = [0]

            def nps():
                p = pR[rr[0] % 3]
                rr[0] += 1
                return p

            def nob():
                o = ob[obr[0] % 4], of[obr[0] % 4]
                obr[0] += 1
                return o

            def ntt():
                t = t1[tr_[0] % 2], t2[tr_[0] % 2]
                tr_[0] += 1
                return t

            pTb = pT.bitcast(BF)
            blocks = []
            for q in self.seqs:
                bs = 512 if q["n"] >= 512 else q["n"]
                for t0 in range(0, q["n"], bs):
                    blocks.append((q, t0, bs))
            vrr = [0]

            def loads(bi):
                q, t0, nt = blocks[bi]
                for j in range(nt // 128):
                    self.dma(xt[j], q["xin_cur"][t0 + j * 128:t0 + (j + 1) * 128, :])
                if not q["prompt"]:
                    self.dma(ropeC[bi % 2], self.c_ropeC[:, t0:t0 + nt])
                    self.dma(ropeS[bi % 2], self.c_ropeS[:, t0:t0 + nt])

            def chain(bi):
                q, t0, nt = blocks[bi]
                ci = 1 if q["prompt"] else 0
                nj_ = nt // 128
                self.norm_batch(xt[0:nj_], mr[(ci, 1)], mr[(ci, 0)], hn[0:nj_], ss4, sd4)

            pTbs = [p.bitcast(BF) for p in pT2]

            def transposes(bi):
                q, t0, nt = blocks[bi]
                h_ = hT[bi % 2]
                for j in range(nt // 128):
                    ptb = pTbs[j % 2]
                    for k in range(8):
                        self.tr(ptb[:, k * 128:(k + 1) * 128], hn[j][:, k * 128:(k + 1) * 128], self.ident_b)
                    self.cp("act", h_[:, :, j * 128:(j + 1) * 128], ptb.re("p (k t) -> p k t", k=8), join=(j > 0))

            loads(0)
            chain(0)
            transposes(0)
            for bi, (q, t0, nt) in enumerate(blocks):
                prompt = q["prompt"]
                koff = q["koff"]
                tk = slice(koff + t0, koff + t0 + nt)
                tq = slice(t0, t0 + nt)
                hTb = hT[bi % 2]
                rC, rS = ropeC[bi % 2], ropeS[bi % 2]
                more = bi + 1 < len(blocks)
                if more:
                    loads(bi + 1)

                def proj(pout, m, c0, W=Win):
                    for k in range(8):
                        self.mm(pout[0:m, 0:nt], W[:, k, c0:c0 + m], hTb[:, k, 0:nt], start=(k == 0), stop=(k == 7))

                proj(pQ[0], 128, O_CQ)
                proj(pQ[1], 64, O_CQ + 128)
                proj(pC, 128, O_CKV)
                pk = nps()
                proj(pk, 32, O_KR)
                self.act(sq[:, 0, 0:nt], pQ[0][:, 0:nt], AF.Square)
                self.act(sq[0:64, 1, 0:nt], pQ[1][0:64, 0:nt], AF.Square, join=True)
                self.act(sq2[:, 0:nt], pC[:, 0:nt], AF.Square)
                if not prompt:
                    pp = nps()
                    proj(pp, 32, 512, WinP)
                    a1, a2 = ntt()
                    self.tt("dve", a1[0:32, 0:nt], pk[0:32, 0:nt], rC[0:32, 0:nt], ALU.mult)
                    self.tt("dve", a2[0:32, 0:nt], pp[0:32, 0:nt], rS[0:32, 0:nt], ALU.mult)
                    self.tt("dve", krT[:, 0:nt], a1[0:32, 0:nt], a2[0:32, 0:nt], ALU.add)
                else:
                    self.cp("act", krf[:, 0:nt], pk[0:32, 0:nt])
                    self.cp("pool", krT[:, 0:nt], krf[:, 0:nt])
                if more:
                    chain(bi + 1)
                for (c0, dst) in ((O_XB, "LXB"), (O_GB, "LGB"), (O_PU, "PU")):
                    for cc in range(2):
                        p = nps()
                        proj(p, 128, c0 + cc * 128)
                        _, o = nob()
                        self.cp("act", o[:, 0:nt], p[:, 0:nt])
                        self.dma(q[dst][cc * 128:(cc + 1) * 128, tq], o[:, 0:nt], q="sp", join=True)
                pSS = nps()
                self.mm(pSS[:, 0:nt], self.ones_b[:, :], sq[:, 0, 0:nt], start=True, stop=False)
                self.mm(pSS[:, 0:nt], self.ones_b[0:64, :], sq[0:64, 1, 0:nt], start=False, stop=True)
                self.act(rst[:, 0:nt], pSS[:, 0:nt], AF.Sqrt, bias=self.eps_col, scale=1.0 / 192)
                pSS = nps()
                self.mm(pSS[:, 0:nt], self.ones_b[:, :], sq2[:, 0:nt])
                self.act(rst2[:, 0:nt], pSS[:, 0:nt], AF.Sqrt, bias=self.eps_col, scale=1.0 / 128)
                self.recip(rst[:, 0:nt], rst[:, 0:nt])
                self.recip(rst2[:, 0:nt], rst2[:, 0:nt])
                self.stt("dve", cqn[:, 0, 0:nt], pQ[0][:, 0:nt], gq[:, 0:1], rst[:, 0:nt], ALU.mult, ALU.mult)
                self.stt("dve", cqn[0:64, 1, 0:nt], pQ[1][0:64, 0:nt], gq[0:64, 1:2], rst[0:64, 0:nt], ALU.mult, ALU.mult, join=True)
                if prompt:
                    self.stt("dve", latf[:, 0:nt], pC[:, 0:nt], gkv[:, 0:1], rst2[:, 0:nt], ALU.mult, ALU.mult)
                    self.cp("pool", lat[:, 0:nt], latf[:, 0:nt])
                else:
                    self.stt("dve", lat[:, 0:nt], pC[:, 0:nt], gkv[:, 0:1], rst2[:, 0:nt], ALU.mult, ALU.mult)
                for (c0, dstn, sl, pc0) in ((O_DQ, "DQ", tq, 0), (O_DK, "DK", tk, 256)):
                    for cc in range(2):
                        pm = nps()
                        proj(pm, 128, c0 + cc * 128)
                        o, ofp = nob()
                        if not prompt:
                            pp = nps()
                            proj(pp, 128, pc0 + cc * 128, WinP)
                            a1, a2 = ntt()
                            self.tt("dve", a1[:, 0:nt], pm[:, 0:nt], rC[:, 0:nt], ALU.mult)
                            self.tt("dve", a2[:, 0:nt], pp[:, 0:nt], rS[:, 0:nt], ALU.mult)
                            self.tt("dve", o[:, 0:nt], a1[:, 0:nt], a2[:, 0:nt], ALU.add)
                        else:
                            if dstn == "DK":
                                self.cp("act", ofp[:, 0:nt], pm[:, 0:nt])
                                self.cp("pool", o[:, 0:nt], ofp[:, 0:nt])
                                for j in range(nt // 128):
                                    pt = nps()
                                    self.tr(pt[:, 0:128], ofp[:, j * 128:(j + 1) * 128], self.ident_f)
                                    v = vf[vrr[0] % 2]
                                    vrr[0] += 1
                                    self.cp("act", v[:, 0:128], pt[:, 0:128])
                                    self.dma(self.o_dk[q["pi"], L, j * 128:(j + 1) * 128, cc * 128:(cc + 1) * 128], v[:, 0:128], q="sp", join=True)
                            else:
                                self.cp("act", o[:, 0:nt], pm[:, 0:nt])
                        self.dma(q[dstn][cc * 128:(cc + 1) * 128, sl], o[:, 0:nt], q="sp", join=True)
                for hh in range(4):
                    pm = nps()
                    self.mm(pm[0:96, 0:nt], Wq[:, 0, hh, :], cqn[:, 0, 0:nt], start=True, stop=False)
                    self.mm(pm[0:96, 0:nt], Wq[0:64, 1, hh, :], cqn[0:64, 1, 0:nt], start=False, stop=True)
                    o, _ = nob()
                    if not prompt:
                        pp = nps()
                        self.mm(pp[0:96, 0:nt], WqP[:, 0, hh, :], cqn[:, 0, 0:nt], start=True, stop=False)
                        self.mm(pp[0:96, 0:nt], WqP[0:64, 1, hh, :], cqn[0:64, 1, 0:nt], start=False, stop=True)
                        a1, a2 = ntt()
                        self.tt("dve", a1[64:96, 0:nt], pm[64:96, 0:nt], rC[64:96, 0:nt], ALU.mult)
                        self.tt("dve", a2[64:96, 0:nt], pp[64:96, 0:nt], rS[64:96, 0:nt], ALU.mult)
                        self.tt("dve", o[64:96, 0:nt], a1[64:96, 0:nt], a2[64:96, 0:nt], ALU.add)
                        self.cp("act", o[0:64, 0:nt], pm[0:64, 0:nt], join=True)
                    else:
                        self.cp("act", o[0:96, 0:nt], pm[0:96, 0:nt])
                    self.dma(q["QTm"][hh, :, tq], o[0:96, 0:nt], q="sp", join=True)
                self.emit_kv(q, lat, krT, nt, tk, Wk, Wv, Isel, nps, nob, vb, vrr)
                if prompt:
                    pi = q["pi"]
                    for j in range(nt // 128):
                        pt = nps()
                        self.tr(pt[:, 0:128], latf[:, j * 128:(j + 1) * 128], self.ident_f)
                        self.tr(pt[:, 128:160], krf[:, j * 128:(j + 1) * 128], self.ident_f[0:32, 0:32])
                        _, o = nob()
                        self.cp("act", o[:, 0:160], pt[:, 0:160])
                        self.dma(self.o_ckv[pi, L, j * 128:(j + 1) * 128, :], o[:, 0:128], q="sp", join=True)
                        self.dma(self.o_kr[pi, L, j * 128:(j + 1) * 128, :], o[:, 128:160], q="sp", join=True)
                for j in range(nt // 128):
                    p = nps()
                    for k in range(8):
                        self.mm(p[:, 0:256], hTb[:, k, j * 128:(j + 1) * 128], Win[:, k, O_DV:O_DV + 256], start=(k == 0), stop=(k == 7))
                    v = vb[vrr[0] % 2]
                    vrr[0] += 1
                    self.cp("act", v, p[:, 0:256])
                    self.dma(q["DV"][koff + t0 + j * 128:koff + t0 + (j + 1) * 128, :], v, q="sp", join=True)
                    if prompt:
                        v2 = vf[vrr[0] % 2]
                        self.cp("dve", v2, p[:, 0:256])
                        self.dma(self.o_dv[q["pi"], L, j * 128:(j + 1) * 128, :], v2, q="sp", join=True)
                if more:
                    transposes(bi + 1)
            q = self.seqs[0]
            self.dma(cst_f[:, :, 0:128], T(self.ckv.ap[L].rearrange("(j p) e -> p j e", p=128), self.ckv.buf))
            self.dma(cst_f[:, :, 128:160], T(self.ckr.ap[L].rearrange("(j p) e -> p j e", p=128), self.ckr.buf), join=True)
            self.cp("dve", cst_b[:, :, 0:160], cst_f[:, :, 0:160])
            for j in range(4):
                self.tr(pTb[:, j * 128:(j + 1) * 128], cst_b[:, j, 0:128], self.ident_b)
                self.tr(pTb[0:32, 512 + j * 128:512 + (j + 1) * 128], cst_b[:, j, 128:160], self.ident_b)
            self.cp("act", lat, pTb[:, 0:512])
            self.cp("act", krT, pTb[0:32, 512:1024])
            self.emit_kv(q, lat, krT, 512, slice(0, 512), Wk, Wv, Isel, nps, nob, vb, vrr)
            self.dma(cst_f, T(self.cdk.ap[L].rearrange("(j p) e -> p j e", p=128), self.cdk.buf))
            self.cp("dve", cst_b, cst_f)
            for cc in range(2):
                for j in range(4):
                    self.tr(pTb[:, j * 128:(j + 1) * 128], cst_b[:, j, cc * 128:(cc + 1) * 128], self.ident_b)
                o, _ = nob()
                self.cp("act", o, pTb[:, 0:512])
                self.dma(q["DK"][cc * 128:(cc + 1) * 128, 0:512], o, q="sp", join=True)
            self.dma(cst_f, T(self.cdv.ap[L].rearrange("(j p) e -> p j e", p=128), self.cdv.buf))
            self.cp("dve", cst_b, cst_f)
            self.dma(T(q["DV"].ap[0:512, :].rearrange("(j p) e -> p j e", p=128), q["DV"].buf), cst_b, q="sp", join=True)

    def emit_kv(self, q, lat, krT, nt, tk, Wk, Wv, Isel, nps, nob, vb, vrr):
        for hh in range(4):
            p = nps()
            self.mm(p[0:96, 0:nt], Wk[:, hh, :], lat[:, 0:nt], start=True, stop=False)
            self.mm(p[0:96, 0:nt], Isel[:, :], krT[:, 0:nt], start=False, stop=True)
            o, _ = nob()
            self.cp("act", o[0:96, 0:nt], p[0:96, 0:nt])
            self.dma(q["KTm"][hh, :, tk], o[0:96, 0:nt], q="sp", join=True)
        for j in range(nt // 128):
            p = nps()
            self.mm(p[:, 0:256], lat[:, j * 128:(j + 1) * 128], Wv.re("p h e -> p (h e)"))
            v = vb[vrr[0] % 2]
            vrr[0] += 1
            self.cp("act", v, p[:, 0:256])
            self.dma(q["Vm"][tk.start + j * 128:tk.start + (j + 1) * 128, :], v, q="sp", join=True)

    def attn_begin(self, pSs, PTs):
        self.at = dict(pSs=pSs, PTs=PTs, st=0, pt=0, pend=[], later=[])

    def attn_tick(self, force=False):
        a = self.at
        for it in a["later"]:
            it[0] -= 1
        while a["later"] and (force or a["later"][0][0] <= 0):
            a["later"].pop(0)[1]()

    def attn_flush(self, keep=0):
        a = self.at
        while len(a["pend"]) > keep:
            kc0, PT, pO, qbs, Vaug, nkc, cb = a["pend"].pop(0)
            for u in range(2):
                kc = kc0 + u
                self.mm(pO[:, 0:qbs], Vaug(kc), PT[:, u * qbs:(u + 1) * qbs], start=(kc == 0), stop=(kc == nkc - 1))
            if kc0 + 2 >= nkc and cb is not None:
                f = cb()
                if f is not None:
                    a["later"].append([8, f])
        if keep == 0:
            self.attn_tick(force=True)

    def attn_block(self, KT, QT, Vaug, kdim, q0, qbs, nk, scale, pO, cb):
        a = self.at
        nkc = nk // 128
        for kc0 in range(0, nkc, 2):
            pS = a["pSs"][a["st"] % len(a["pSs"])]
            PT = a["PTs"][a["pt"] % len(a["PTs"])]
            a["st"] += 1
            a["pt"] += 1
            for u in range(2):
                kc = kc0 + u
                self.mm(pS[:, u * qbs:(u + 1) * qbs], KT[0:kdim, kc * 128:(kc + 1) * 128], QT[0:kdim, q0:q0 + qbs])
            self.act(PT[:, 0:2 * qbs], pS[:, 0:2 * qbs], AF.Exp, scale=scale)
            a["pend"].append((kc0, PT, pO, qbs, Vaug, nkc, cb))
            self.attn_flush(keep=2)
            self.attn_tick()

    def phase2_mla(self, L):
        with ExitStack() as es:
            sb = lambda n, s, dt=F32: self.sb(es, n, s, dt)
            KT = [sb(f"KT{i}", [96, NS + PAST], BF) for i in range(2)]
            QT = [sb(f"QT{i}", [96, NS], BF) for i in range(2)]
            Va = self.Va
            PTs = [sb(f"PT{i}", [128, 1024], BF) for i in range(4)]
            rec = [sb(f"rec{i}", [64, 512]) for i in range(2)]
            yo = [sb(f"yo{i}", [64, 512], BF) for i in range(2)]
            pSs = [self.ps(es, f"pS{i}", [128, 1024]) for i in range(3)]
            pOs = [self.ps(es, f"pO{i}") for i in range(2)]
            cnt = 0
            oi = 0
            self.attn_begin(pSs, PTs)
            pg = self.pool_gen(L, es)
            scale = 1.0 / math.sqrt(96.0)
            for q in self.seqs:
                nq, nk = q["n"], q["nk"]
                nkc = nk // 128
                self.attn_flush()
                for h_ in range(4):
                    vsrc = T(q["Vm"].ap[:, h_ * 64:(h_ + 1) * 64].rearrange("(c p) e -> p c e", p=128), q["Vm"].buf)
                    for c0_ in range(0, nkc, 8):
                        c1_ = min(nkc, c0_ + 8)
                        self.dma(Va[:, c0_:c1_, h_, 0:64], vsrc[:, c0_:c1_, :], join=(h_ > 0 or c0_ > 0))
                for hh in range(4):
                    kt, qt = KT[cnt % 2], QT[cnt % 2]
                    cnt += 1
                    self.dma(kt[:, 0:nk], q["KTm"][hh])
                    self.dma(qt[:, 0:nq], q["QTm"][hh])
                    qbs = 512 if nq >= 512 else nq
                    for q0 in range(0, nq, qbs):
                        pO = pOs[oi % 2]
                        r, y = rec[oi % 2], yo[oi % 2]
                        oi += 1
                        next(pg, None)
                        next(pg, None)

                        def epi(pO=pO, r=r, y=y, q=q, hh=hh, q0=q0, qbs=qbs):
                            self.recip(r[:, 0:qbs], pO[64:128, 0:qbs])
                            self.tt("dve", y[:, 0:qbs], pO[0:64, 0:qbs], r[:, 0:qbs], ALU.mult)
                            self.dma(q["YT"][hh * 64:(hh + 1) * 64, q0:q0 + qbs], y[:, 0:qbs], q="pool")

                        self.attn_block(kt, qt, lambda kc, hh=hh: Va[:, kc, hh, :], 96, q0, qbs, nk, scale, pO, epi)
            self.attn_flush()
            for _ in pg:
                pass

    def phase3_diff(self, L):
        lam_init = 0.8 - 0.6 * math.exp(-0.3 * L)
        with ExitStack() as es:
            sb = lambda n, s, dt=F32: self.sb(es, n, s, dt)
            KT = [sb(f"dKT{i}", [128, NS + PAST], BF) for i in range(2)]
            Qz = [[sb(f"dQz{b}_{i}", [128, NS], BF) for i in range(4)] for b in range(2)]
            Va = self.Va
            PTs = [sb(f"dPT{i}", [128, 1024], BF) for i in range(4)]
            rec = sb("drec", [64, 512])
            o0 = sb("do0", [64, 512])
            o1 = sb("do1", [64, 512])
            osq = [sb(f"dosq{i}", [64, 512], BF) for i in range(2)]
            odq = [sb(f"dodq{i}", [64, 512]) for i in range(2)]
            gcol2 = sb("dgcol2", [64, 1])
            rs = sb("drs", [64, 512])
            yo = [sb(f"dyo{i}", [64, 512], BF) for i in range(2)]
            lv = sb("dlv", [128, 4, 32])
            lp = sb("dlp", [128, 2, 32])
            ls = sb("dls", [128, 2])
            nlam = sb("dnlam", [128, 1])
            gcol = sb("dgcol", [64, 1])
            epsc = sb("depsc", [128, 1])
            pSs = [self.ps(es, f"dpS{i}", [128, 1024]) for i in range(3)]
            pOs = [self.ps(es, f"dpO{i}") for i in range(2)]
            for b in range(2):
                for i in range(4):
                    self.memset("pool" if i % 2 else "dve", Qz[b][i], 0.0)
            self.memset("pool", epsc, EPS)
            self.dma(lv.re("p a b -> p (a b)"), T(self.diff_lambda.ap[L:L + 1, :].partition_broadcast(128), self.diff_lambda.buf))
            lv4 = lv.re("p (a b) e -> p a b e", b=2)
            self.tt("dve", lp, lv4[:, :, 0, :], lv4[:, :, 1, :], ALU.mult)
            self.S.op("dve", lambda e: e.tensor_reduce(ls.ap, lp.ap, AX.X, ALU.add), [lp.buf], [ls.buf])
            self.act(ls, ls, AF.Exp)
            self.tt("dve", nlam, ls[:, 1:2], ls[:, 0:1], ALU.subtract)
            self.ts("dve", nlam, nlam, -lam_init, ALU.add)
            self.dma(gcol, T(self.diff_norm_g.ap[L, :].rearrange("(p o) -> p o", o=1), self.diff_norm_g.buf))
            self.ts("dve", gcol, gcol, 1.0 - lam_init, ALU.mult)
            self.ts("dve", gcol2, gcol, 0.5, ALU.mult)
            cnt = 0
            oi = 0
            yi = 0
            self.attn_begin(pSs, PTs)
            scale = 1.0 / math.sqrt(32.0)
            for q in self.seqs:
                nq, nk = q["n"], q["nk"]
                nkc = nk // 128
                self.attn_flush()
                for h_ in range(4):
                    vsrc = T(q["DV"].ap[:, h_ * 64:(h_ + 1) * 64].rearrange("(c p) e -> p c e", p=128), q["DV"].buf)
                    for c0_ in range(0, nkc, 8):
                        c1_ = min(nkc, c0_ + 8)
                        self.dma(Va[:, c0_:c1_, h_, 0:64], vsrc[:, c0_:c1_, :], join=(h_ > 0 or c0_ > 0))
                for hp in range(2):
                    kt, qz = KT[cnt % 2], Qz[cnt % 2]
                    cnt += 1
                    self.dma(kt[:, 0:nk], q["DK"][hp * 128:(hp + 1) * 128, :])
                    for i in range(4):
                        self.dma(qz[i][32 * i:32 * i + 32, 0:nq], q["DQ"][hp * 128 + 32 * i:hp * 128 + 32 * i + 32, :], join=True)
                    qbs = 512 if nq >= 512 else nq
                    for hl in range(2):
                        hh = 2 * hp + hl
                        for q0 in range(0, nq, qbs):
                            self.precast_step()
                            pA, pB = pOs[oi % 2], pOs[(oi + 1) % 2]
                            oi += 2
                            y = yo[yi % 2]
                            yi += 1

                            def epi0(pA=pA, qbs=qbs):
                                self.recip(rec[:, 0:qbs], pA[64:128, 0:qbs])
                                self.tt("dve", o0[:, 0:qbs], pA[0:64, 0:qbs], rec[:, 0:qbs], ALU.mult)

                            def epi1(pB=pB, qbs=qbs, y=y, q=q, hh=hh, q0=q0, yi_=yi):
                                self.attn_tick(force=True)
                                self.recip(rec[:, 0:qbs], pB[64:128, 0:qbs])
                                self.tt("dve", o1[:, 0:qbs], pB[0:64, 0:qbs], rec[:, 0:qbs], ALU.mult)
                                self.stt("dve", o0[:, 0:qbs], o1[:, 0:qbs], nlam[0:64, 0:1], o0[:, 0:qbs], ALU.mult, ALU.add)
                                od, sq_ = odq[yi_ % 2], osq[yi_ % 2]
                                self.tt("pool", od[:, 0:qbs], o0[:, 0:qbs], o0[:, 0:qbs], ALU.add)
                                self.tt("pool", sq_[:, 0:qbs], o0[:, 0:qbs], o0[:, 0:qbs], ALU.mult)

                                def epi2():
                                    a = self.at
                                    pN = a["pSs"][a["st"] % len(a["pSs"])]
                                    a["st"] += 1
                                    self.mm(pN[0:64, 0:qbs], self.ones_b[0:64, 0:64], sq_[:, 0:qbs])
                                    self.act(rs[:, 0:qbs], pN[0:64, 0:qbs], AF.Ln, bias=epsc[0:64, :], scale=1.0 / 64)
                                    self.act(rs[:, 0:qbs], rs[:, 0:qbs], AF.Exp, scale=-0.5)
                                    self.stt("dve", y[:, 0:qbs], od[:, 0:qbs], gcol2[:, 0:1], rs[:, 0:qbs], ALU.mult, ALU.mult)
                                    self.dma(q["YT"][768 + hh * 64:768 + (hh + 1) * 64, q0:q0 + qbs], y[:, 0:qbs], q="pool")

                                return epi2

                            for c, (pO, cb) in enumerate(((pA, epi0), (pB, epi1))):
                                self.attn_block(kt, qz[2 * hl + c], lambda kc, hh=hh: Va[:, kc, hh, :], 128, q0, qbs, nk, scale, pO, cb)


            self.attn_flush()

    def phase4_lru(self, L):
        with ExitStack() as es:
            sb = lambda n, s, dt=F32: self.sb(es, n, s, dt)
            W = NS + 3
            X = sb("lX", [128, W])
            XC = sb("lXC", [128, NS])
            XCb = sb("lXCb", [128, NS], BF)
            A0 = sb("lA0", [128, NS])
            B0 = sb("lB0", [128, NS])
            S0 = sb("lS0", [128, NS])
            B1 = sb("lB1", [128, NS])
            S1 = sb("lS1", [128, NS])
            H0 = sb("lH0", [128, NS])
            Yb = sb("lYb", [128, NS], BF)
            wst = sb("lwst", [128, 8, 128])
            Wg = sb("lWg", [128, 8, 128], BF)
            cw = sb("lcw", [128, 4, 2])
            cbias = sb("lcb", [128, 2])
            gb = sb("lgb", [128, 2, 2, 2])
            lam = sb("llam", [128, 2, 2])
            cA = sb("lcA", [128, 2, 2])
            h0 = sb("lh0", [128, 2, 2])
            fin = sb("lfin", [128, 2])
            pG = [self.ps(es, f"lpG{i}") for i in range(4)]
            self.memset("pool", wst, 0.0)
            for gi, wsrc in enumerate((self.lru_w_r, self.lru_w_i)):
                for d in range(2):
                    for g4 in range(4):
                        cc, hf = g4 // 2, g4 % 2
                        idx = (gi * 2 + d) * 2 + cc
                        self.dma(wst[hf * 64:(hf + 1) * 64, idx, hf * 64:(hf + 1) * 64], wsrc[L, d, g4], join=True)
            self.cp("dve", Wg, wst)
            self.dma(cw, T(self.lru_conv_w.ap[L].rearrange("j (c p) -> p j c", p=128), self.lru_conv_w.buf), allow_slow_non_contiguous=True)
            self.load_cols(cbias, self.lru_conv_b[L], 2)
            self.dma(gb[:, 0], T(self.lru_b_r.ap[L].rearrange("d (c p) -> p d c", p=128), self.lru_b_r.buf), allow_slow_non_contiguous=True)
            self.dma(gb[:, 1], T(self.lru_b_i.ap[L].rearrange("d (c p) -> p d c", p=128), self.lru_b_i.buf), join=True, allow_slow_non_contiguous=True)
            self.dma(lam, T(self.lru_lambda.ap[L].rearrange("d (c p) -> p d c", p=128), self.lru_lambda.buf), allow_slow_non_contiguous=True)
            self.dma(h0, T(self.cst.ap[L].rearrange("d (c p) -> p d c", p=128), self.cst.buf), allow_slow_non_contiguous=True)
            self.act(cA, lam, AF.Exp, scale=-1.0)
            self.act(cA, cA, AF.Ln, bias=1.0)
            self.ts("dve", cA, cA, -LRU_C, ALU.mult)
            nl = getattr(self, "nlayers", DEPTH)
            p0 = self.p0_gen(L + 1, es) if L + 1 < nl else iter(())
            bufs = {"S": dict(X=X, XC=XC, XCb=XCb, A0=A0, B0=B0, S0=S0, B1=B1, S1=S1, H0=H0, Yb=Yb, fin=fin)}
            for nm in ("P0", "P1"):
                d_ = dict(X=sb("lX" + nm, [128, NP + 3]), XCb=sb("lXCb" + nm, [128, NP], BF), Yb=sb("lYb" + nm, [128, NP], BF),
                          fin=sb("lfin" + nm, [128, 2]))
                for k_ in ("XC", "A0", "B0", "S0", "B1", "S1", "H0"):
                    d_[k_] = sb("l" + k_ + nm, [128, NP])
                bufs[nm] = d_
            gctr = [0]

            def chain(q, cc):
                n = q["n"]
                bs = 512 if n >= 512 else n
                b_ = bufs[q["name"]]
                X, XC, XCb, A0, B0, S0, B1, S1, H0, Yb, fin = (b_[k_] for k_ in ("X", "XC", "XCb", "A0", "B0", "S0", "B1", "S1", "H0", "Yb", "fin"))
                A1 = T(X.ap[:, 0:n], X.buf)
                Ad, Bd, Sd = (A0, A1), (B0, B1), (S0, S1)
                H1, G, G2 = A0, S0, S1
                self.memset("pool", X[:, 0:1], 0.0)
                self.memset("pool", X[:, n + 1:n + 3], 0.0, join=True)
                self.dma(X[:, 1:n + 1], q["LXB"][cc * 128:(cc + 1) * 128, :], join=True)
                yield
                self.ts("dve", XC[:, 0:n], X[:, 0:n], cw[:, 0, cc:cc + 1], ALU.mult, cbias[:, cc:cc + 1], ALU.add)
                for j in range(1, 4):
                    self.stt("dve", XC[:, 0:n], X[:, j:j + n], cw[:, j, cc:cc + 1], XC[:, 0:n], ALU.mult, ALU.add)
                self.cp("act", XCb[:, 0:n], XC[:, 0:n])
                yield
                for bi, b0 in enumerate(range(0, n, bs)):
                    for d in range(2):
                        for gi, dst in ((0, Ad[d]), (1, Bd[d])):
                            idx = (gi * 2 + d) * 2 + cc
                            p = pG[gctr[0] % 4]
                            gctr[0] += 1
                            self.mm(p[:, 0:bs], Wg[:, idx, :], XCb[:, b0:b0 + bs])
                            self.act(dst[:, b0:b0 + bs], p[:, 0:bs], AF.Sigmoid, bias=gb[:, gi, d, cc:cc + 1], join=(bi > 0))
                    if bi % 2 == 1:
                        yield
                yield
                for d in range(2):
                    self.act(Ad[d][:, 0:n], Ad[d][:, 0:n], AF.Exp, scale=cA[:, d, cc:cc + 1])
                yield
                for d in range(2):
                    self.act(Sd[d][:, 0:n], Ad[d][:, 0:n], AF.Square)
                    self.tt("dve", Bd[d][:, 0:n], Bd[d][:, 0:n], XC[:, 0:n], ALU.mult)
                yield
                for d in range(2):
                    self.act(Sd[d][:, 0:n], Sd[d][:, 0:n], AF.Ln, bias=1.0, scale=-1.0)
                yield
                for d in range(2):
                    self.act(Sd[d][:, 0:n], Sd[d][:, 0:n], AF.Exp, scale=0.5)
                yield
                for d in range(2):
                    self.tt("dve", Bd[d][:, 0:n], Bd[d][:, 0:n], Sd[d][:, 0:n], ALU.mult)
                yield
                self.dma(G[:, 0:n], q["LGB"][cc * 128:(cc + 1) * 128, :])
                init0 = 0.0 if q["prompt"] else h0[:, 0, cc:cc + 1]
                init1 = 0.0 if q["prompt"] else h0[:, 1, cc:cc + 1]
                self.scan(H0[:, 0:n], A0[:, 0:n], B0[:, 0:n], init0)
                yield
                self.act(G2[:, 0:n], G[:, 0:n], AF.Square)
                self.act(G2[:, 0:n], G2[:, 0:n], AF.Identity, bias=1.0, scale=0.044715)
                self.tt("dve", G2[:, 0:n], G2[:, 0:n], G[:, 0:n], ALU.mult)
                yield
                self.act(G2[:, 0:n], G2[:, 0:n], AF.Sigmoid, scale=1.5957691216057308)
                rv = lambda t: T(AP(t.ap.tensor, t.ap.offset + n - 1, [list(t.ap.ap[0]), [-1, n]]), t.buf)
                self.scan(rv(H1), rv(A1), rv(B1), init1)
                yield
                self.tt("dve", G[:, 0:n], G[:, 0:n], G2[:, 0:n], ALU.mult)
                if q["prompt"]:
                    self.cp("dve", fin[:, 0:1], H0[:, n - 1:n])
                    self.cp("dve", fin[:, 1:2], H1[:, 0:1], join=True)
                    dst = T(self.o_st.ap[q["pi"], L, :, cc * 128:(cc + 1) * 128].rearrange("d p -> p d"), self.o_st.buf)
                    self.dma(dst, fin, q="pool", join=True, allow_slow_non_contiguous=True)
                yield
                self.tt("dve", H0[:, 0:n], H0[:, 0:n], H1[:, 0:n], ALU.add)
                self.tt("dve", Yb[:, 0:n], H0[:, 0:n], G[:, 0:n], ALU.mult)
                self.dma(q["YT"][256 + cc * 128:256 + (cc + 1) * 128, :], Yb[:, 0:n], q="pool", join=True)

            for cc in range(2):
                gens = [chain(q, cc) for q in self.seqs]
                while gens:
                    for g_ in list(gens):
                        try:
                            next(g_)
                        except StopIteration:
                            gens.remove(g_)
                    next(p0, None)
            for _ in p0:
                pass

    def pool_gen(self, L, es):
        PAD = 16
        sb = lambda n, s, dt=F32: self.sb(es, n, s, dt)
        W = NS + 2 * PAD
        X = sb("pX", [128, W])
        SA = sb("pSA", [128, W])
        SB = sb("pSB", [128, W])
        inv = sb("pinv", [128, NS])
        Db = sb("pDb", [128, NS], BF)
        yb = [sb(f"pyb{i}", [128, 512], BF) for i in range(2)]
        wst = sb("pwst", [128, 2, 128])
        Wp = sb("pWp", [128, 2, 128], BF)
        psc = sb("ppsc", [128, 2])
        self.memset("pool", wst, 0.0)
        for g4 in range(4):
            cc, hf = g4 // 2, g4 % 2
            self.dma(wst[hf * 64:(hf + 1) * 64, cc, hf * 64:(hf + 1) * 64], self.pool_w[L, g4], join=True)
        self.cp("pool", Wp, wst)
        self.load_cols(psc, self.pool_scale[L], 2)
        yield
        yi = 0
        for q in self.seqs:
            n = q["n"]
            pinv = self.c_pinvP if q["prompt"] else self.c_pinvS
            for cc in range(2):
                we = n + 2 * PAD
                self.memset("pool", X[:, 0:PAD], 0.0)
                self.memset("pool", X[:, PAD + n:we], 0.0, join=True)
                self.dma(X[:, PAD:PAD + n], q["PU"][cc * 128:(cc + 1) * 128, :], join=True)
                for hf in range(2):
                    g4 = cc * 2 + hf
                    self.dma(inv[hf * 64:(hf + 1) * 64, 0:n], T(pinv.ap[g4:g4 + 1, :].partition_broadcast(64), pinv.buf), join=(hf > 0))
                yield
                self.tt("dve", SA[:, 1:we - 1], X[:, 0:we - 2], X[:, 1:we - 1], ALU.add)
                yield
                self.tt("dve", SB[:, 2:we - 2], SA[:, 1:we - 3], SA[:, 3:we - 1], ALU.add)
                yield
                if cc == 1:
                    self.tt("dve", SA[:, 4:we - 4], SB[:, 2:we - 6], SB[:, 6:we - 2], ALU.add)
                    yield
                    self.tt("dve", SB[:, 8:we - 8], SA[:, 4:we - 12], SA[:, 12:we - 4], ALU.add)
                    yield
                self.tt("dve", SA[0:64, PAD:PAD + n], SA[0:64, PAD:PAD + n], inv[0:64, 0:n], ALU.mult)
                self.tt("dve", SB[64:128, PAD:PAD + n], SB[64:128, PAD:PAD + n], inv[64:128, 0:n], ALU.mult)
                yield
                self.tt("dve", Db[0:64, 0:n], SA[0:64, PAD:PAD + n], X[0:64, PAD:PAD + n], ALU.subtract)
                self.tt("dve", Db[64:128, 0:n], SB[64:128, PAD:PAD + n], X[64:128, PAD:PAD + n], ALU.subtract, join=True)
                yield
                yield
                bs = 512 if n >= 512 else n
                for bi, b0 in enumerate(range(0, n, bs)):
                    a = self.at
                    p = a["pSs"][a["st"] % len(a["pSs"])]
                    a["st"] += 1
                    self.mm(p[:, 0:bs], Wp[:, cc, :], Db[:, b0:b0 + bs])
                    y = yb[yi % 2]
                    yi += 1
                    self.ts("dve", y[:, 0:bs], p[:, 0:bs], psc[:, cc:cc + 1], ALU.mult)
                    self.dma(q["YT"][512 + cc * 128:512 + (cc + 1) * 128, b0:b0 + bs], y[:, 0:bs], q="pool")
                    if bi % 2 == 1:
                        yield

    def phase6_outproj(self, L):
        with ExitStack() as es:
            sb = lambda n, s, dt=F32: self.sb(es, n, s, dt)
            Wo = sb("Wo", [128, 8, D], BF)
            yT = [sb(f"oyT{i}", [128, 8, 512], BF) for i in range(2)]
            xt = [sb(f"oxt{i}", [128, D]) for i in range(3)]
            tmp = [sb(f"otmp{i}", [128, D]) for i in range(3)]
            xo = [sb(f"oxo{i}", [128, D]) for i in range(3)]
            pO = [self.ps(es, f"opO{i}") for i in range(4)]
            self.load_bf16(Wo, self.WoB, D, D)
            mr = self.mod_rows(es, L, (2,), "p6")
            bi = 0
            ti = 0
            for q in self.seqs:
                n = q["n"]
                g1 = mr[(1 if q["prompt"] else 0, 2)]
                bs = 512 if n >= 512 else n
                for b0 in range(0, n, bs):
                    y = yT[bi % 2]
                    bi += 1
                    ysrc = T(q["YT"].ap[:, b0:b0 + bs].rearrange("(k p) t -> p k t", p=128), q["YT"].buf)
                    self.dma(y[:, 0:4, 0:bs], ysrc[:, 0:4, :])
                    self.dma(y[:, 4:8, 0:bs], ysrc[:, 4:8, :], join=True)
                    for j in range(bs // 128):
                        x, tm, o = xt[ti % 3], tmp[ti % 3], xo[ti % 3]
                        r0 = b0 + j * 128
                        self.dma(x, q["xin_cur"][r0:r0 + 128, :])
                        for hf in range(2):
                            p = pO[(2 * ti + hf) % 4]
                            for k in range(8):
                                self.mm(p, y[:, k, j * 128:(j + 1) * 128], Wo[:, k, hf * 512:(hf + 1) * 512], start=(k == 0), stop=(k == 7))
                            self.tt("dve", tm[:, hf * 512:(hf + 1) * 512], p, g1[:, hf * 512:(hf + 1) * 512], ALU.mult, join=(hf > 0))
                        self.tt("dve", o, tm, x, ALU.add)
                        self.dma(q["XA"][r0:r0 + 128, :], o, q="pool")
                        ti += 1

    def phase7_ffn(self, L):
        last = (L == DEPTH - 1)
        HC = 11
        for half in range(2):
            with ExitStack() as es:
                sb = lambda n, s, dt=F32: self.sb(es, n, s, dt)
                Wg = sb("Wg", [128, 8, HC * 128], BF)
                Wu = sb("Wu", [128, 8, HC * 128], BF)
                Wd = sb("Wd", [128, HC, D], BF)
                xt = [sb(f"fxt{i}", [128, D]) for i in range(4)]
                xr = [sb(f"fxr{i}", [128, D]) for i in range(2)]
                junk = sb("fjunk", [128, D], BF)
                tmp = None
                hn = [sb(f"fhn{i}", [128, D], BF) for i in range(4)]
                ss = sb("fss", [128, 1])
                sd = sb("fsd", [128, 1])
                ss4 = sb("fss4", [128, 4])
                sd4 = sb("fsd4", [128, 4])
                self.eps_col = sb("feps", [128, 1])
                self.memset("dve", self.eps_col, EPS)
                hT = [sb(f"fhT{i}", [128, 8, 512], BF) for i in range(2)]
                aT = sb("faT", [128, HC, 512], BF)
                sg = [sb(f"fsg{i}", [128, 512]) for i in range(2)]
                tm = sb("ftm", [128, D])
                xo = [sb(f"fxo{i}", [128, D]) for i in range(2)]
                yo = [sb(f"fyo{i}", [128, D]) for i in range(2)]
                pT = [self.ps(es, f"fpT{i}") for i in range(2)]
                pG = [self.ps(es, f"fpG{i}") for i in range(2)]
                pU = [self.ps(es, f"fpU{i}") for i in range(2)]
                pD = [self.ps(es, f"fpD{i}") for i in range(2)]
                pTb = [p.bitcast(BF) for p in pT]
                h0c = half * HC * 128
                self.load_bf16(Wg, self.WguB[:, h0c:h0c + HC * 128], D, HC * 128)
                self.load_bf16(Wu, self.WguB[:, FF + h0c:FF + h0c + HC * 128], D, HC * 128)
                self.load_bf16(Wd, self.WdB[h0c:h0c + HC * 128, :], HC * 128, D)
                mr = self.mod_rows(es, L, (3, 4, 5), "p7")
                blocks = []
                for q in self.seqs:
                    bs = 512 if q["n"] >= 512 else q["n"]
                    for b0 in range(0, q["n"], bs):
                        blocks.append((q, b0, bs))
                xi = [0]
                tj = [0]

                def norm_loads(bi):
                    q, b0, bs = blocks[bi]
                    for j in range(bs // 128):
                        self.dma(xt[j], q["XA"][b0 + j * 128:b0 + (j + 1) * 128, :])

                def norm_chain(bi):
                    q, b0, bs = blocks[bi]
                    ci = 1 if q["prompt"] else 0
                    nj_ = bs // 128
                    self.norm_batch(xt[0:nj_], mr[(ci, 4)], mr[(ci, 3)], hn[0:nj_], ss4, sd4)

                def transposes(bi):
                    q, b0, bs = blocks[bi]
                    h_ = hT[bi % 2]
                    for j in range(bs // 128):
                        pt = pTb[tj[0] % 2]
                        tj[0] += 1
                        for k in range(8):
                            self.tr(pt[:, k * 128:(k + 1) * 128], hn[j][:, k * 128:(k + 1) * 128], self.ident_b)
                        self.cp("act", h_[:, :, j * 128:(j + 1) * 128], pt.re("p (k t) -> p k t", k=8), join=(j > 0))

                norm_loads(0)
                norm_chain(0)
                transposes(0)
                ti = 0
                for bi, (q, b0, bs) in enumerate(blocks):
                    ci = 1 if q["prompt"] else 0
                    xres = q["XA"] if half == 0 else q["XB"]
                    h_ = hT[bi % 2]
                    nj = bs // 128
                    if bi + 1 < len(blocks):
                        norm_loads(bi + 1)
                    for c in range(HC):
                        if c == 3 and bi + 1 < len(blocks):
                            norm_chain(bi + 1)
                        g, u = pG[c % 2], pU[c % 2]
                        for k in range(8):
                            self.mm(g[:, 0:bs], Wg[:, k, c * 128:(c + 1) * 128], h_[:, k, 0:bs], start=(k == 0), stop=(k == 7))
                        for k in range(8):
                            self.mm(u[:, 0:bs], Wu[:, k, c * 128:(c + 1) * 128], h_[:, k, 0:bs], start=(k == 0), stop=(k == 7))
                        s_ = sg[c % 2]
                        self.act(s_[:, 0:bs], g[:, 0:bs], AF.Silu)
                        self.tt("dve", aT[:, c, 0:bs], s_[:, 0:bs], u[:, 0:bs], ALU.mult, join=(c > 0))
                    if bi + 1 < len(blocks):
                        transposes(bi + 1)
                    for j in range(nj):
                        r0 = b0 + j * 128
                        x = xr[ti % 2]
                        self.dma(x, xres[r0:r0 + 128, :])
                        o = xo[ti % 2]
                        for hf in range(2):
                            p = pD[hf]
                            for c in range(HC):
                                self.mm(p, aT[:, c, j * 128:(j + 1) * 128], Wd[:, c, hf * 512:(hf + 1) * 512], start=(c == 0), stop=(c == HC - 1))
                            self.tt("dve", tm[:, hf * 512:(hf + 1) * 512], p, mr[(ci, 5)][:, hf * 512:(hf + 1) * 512], ALU.mult, join=(hf > 0))
                        self.tt("dve", o, tm, x, ALU.add)
                        if half == 1 and last:
                            y = yo[ti % 2]
                            self.norm_rows(es, o, self.fin_g, None, y, tmp, junk, ss, sd)
                            self.dma(q["yout"][r0:r0 + 128, :], y, q="pool")
                        else:
                            self.dma((q["XB"] if half == 0 else q["XC"])[r0:r0 + 128, :], o, q="pool")
                        ti += 1
            self.S.barrier()
        if not last:
            for q in self.seqs:
                q["xin_cur"] = q["XC"]


def _build(debug_stop=None, dbg=False, nlayers=DEPTH):
    needed = None
    for _pass in range(2):
        mk = MK(debug_stop, dbg=dbg, needed=needed)
        mk.nlayers = nlayers
        for q in mk.seqs:
            q["xin_cur"] = q["xin"]
        mk.build()
        needed = mk.S.rec
    return mk


def _consts():
    rows = NS // GRID_W
    row = np.repeat(np.arange(rows), GRID_W).astype(np.float32)
    col = np.tile(np.arange(GRID_W), rows).astype(np.float32)
    inv = (10000.0 ** (-np.arange(8, dtype=np.float32) / 8)).astype(np.float32)
    ang = np.concatenate([row[:, None] * inv, col[:, None] * inv], axis=-1)
    cos = np.cos(ang).astype(np.float32).T
    sin = np.sin(ang).astype(np.float32).T
    idx = np.arange(128) % 16
    out = {"c_ident": np.eye(128, dtype=np.float32), "c_ropeC": np.ascontiguousarray(cos[idx]), "c_ropeS": np.ascontiguousarray(sin[idx])}
    for name, n in (("c_pinvS", NS), ("c_pinvP", NP)):
        t = np.arange(n)
        a = np.zeros((4, n), np.float32)
        for g, w in enumerate((2, 4, 8, 16)):
            lo = np.clip(t - w // 2, 0, n)
            hi = np.clip(t + w // 2, 0, n)
            a[g] = 1.0 / (hi - lo).astype(np.float32)
        out[name] = a
    return out


_WNAMES = ["w_ada", "b_ada", "norm1_g", "norm2_g", "w_in", "mla_q_norm_g", "mla_w_uq", "mla_kv_norm_g", "mla_w_ukv",
           "lru_conv_w", "lru_conv_b", "lru_w_r", "lru_b_r", "lru_w_i", "lru_b_i", "lru_lambda", "pool_w", "pool_scale",
           "diff_norm_g", "w_out", "w_gu", "w_down"]


def _in_maps(inp):
    c = lambda a: np.ascontiguousarray(np.asarray(a, dtype=np.float32))
    shared = {k: c(inp[k]) for k in _WNAMES}
    shared["diff_lambda"] = c(np.asarray(inp["diff_lambda"]).reshape(DEPTH, 128))
    shared["final_norm_g"] = c(np.asarray(inp["final_norm_g"]).reshape(1, D))
    shared.update(_consts())
    maps = []
    for i in range(8):
        m = dict(shared)
        m["xs"] = c(inp["x_sample"][i])
        m["xp"] = c(np.asarray(inp["x_prompt"][2 * i:2 * i + 2]).reshape(2 * NP, D))
        m["ckv"] = c(inp["cache_mla_ckv"][i])
        m["ckr"] = c(inp["cache_mla_krope"][i])
        m["cdk"] = c(np.asarray(inp["cache_diff_k"][i]).reshape(DEPTH, PAST, 256))
        m["cdv"] = c(np.asarray(inp["cache_diff_v"][i]).reshape(DEPTH, PAST, 256))
        m["cst"] = c(inp["state_lru"][i])
        m["cb"] = c(np.asarray(inp["c"][i]).reshape(1, D))
        m["cctx"] = c(np.asarray(inp["c_ctx"]).reshape(1, D))
        maps.append(m)
    return maps


def kernel(**inp):
    mk = _build()
    res = run_bass_kernel_spmd(mk.nc, _in_maps(inp), core_ids=list(range(8)))
    R = res.results
    f = lambda k: [np.asarray(r[k], dtype=np.float32) for r in R]
    y_sample = np.stack(f("ys"), 0)
    y_prompt = np.concatenate(f("yp"), 0).reshape(16, NP, D)
    ckv = np.concatenate(f("o_ckv"), 0)
    kr = np.concatenate(f("o_kr"), 0)
    dk = np.concatenate(f("o_dk"), 0).reshape(16, DEPTH, NP, 4, 2, 32)
    dv = np.concatenate(f("o_dv"), 0).reshape(16, DEPTH, NP, 4, 64)
    st = np.concatenate(f("o_st"), 0)
    return (y_prompt, y_sample, ckv, kr, dk, dv, st)
```

```python
import math
from contextlib import ExitStack

import numpy as np
import concourse.bass as bass
import concourse.mybir as mybir
from concourse.ap import AP
from concourse.bass_utils import run_bass_kernel_spmd

F32 = mybir.dt.float32
BF = mybir.dt.bfloat16
ALU = mybir.AluOpType
AF = mybir.ActivationFunctionType
AX = mybir.AxisListType

D = 1024
DEPTH = 2
NS = 4096
NP = 256
PAST = 512
GRID_W = 64
IN_COLS = 1888
FF = 2816
EPS = 1e-6
LRU_C = 8.0
O_CQ, O_CKV, O_KR, O_XB, O_GB, O_PU, O_DQ, O_DK, O_DV = 0, 192, 320, 352, 608, 864, 1120, 1376, 1632


class Buf:
    __slots__ = ("w", "r", "base", "name", "psum")

    def __init__(self, name="", psum=False):
        self.psum = psum
        self.w = {}
        self.base = {}
        self.r = {}
        self.name = name


class T:
    __slots__ = ("ap", "buf")

    def __init__(self, ap, buf):
        self.ap = ap
        self.buf = buf

    def __getitem__(self, k):
        return T(self.ap[k], self.buf)

    def re(self, s, **kw):
        return T(self.ap.rearrange(s, **kw), self.buf)

    def bitcast(self, dt):
        return T(self.ap.bitcast(dt), self.buf)


def _ap(x):
    return x.ap if isinstance(x, T) else x


class Sched:
    def __init__(self, nc, n_dma=20, needed=None):
        self.needed = needed
        self.rec = set()
        self.idx = {k: 0 for k in ("pe", "act", "dve", "pool")}
        self.nc = nc
        self.E = {"pe": nc.tensor, "act": nc.scalar, "dve": nc.vector, "pool": nc.gpsimd, "sp": nc.sync}
        self.sem = {k: nc.alloc_semaphore("s_" + k) for k in ("pe", "act", "dve", "pool")}
        self.cnt = {k: 0 for k in self.sem}
        self.waited = {}
        self.dq = {}
        for q in ("sp", "pool", "act"):
            self.dq[q] = dict(sems=[nc.alloc_semaphore(f"d_{q}{i}") for i in range(n_dma)], cnt=[0] * n_dma, rr=0)
        self.n_ins = 0

    def _wait(self, eng, tok):
        sem, val = tok
        key = (eng, sem.name)
        if self.waited.get(key, 0) >= val:
            return
        self.E[eng].wait_ge(sem, val)
        self.waited[key] = val
        self.n_ins += 1
        if self.needed is None:
            self.rec.add((sem.name, val))

    def _maybe(self, eng, tok):
        if eng == "pe" and tok[0] is self.sem["pe"]:
            return
        self._wait(eng, tok)

    def _deps(self, eng, reads, writes, join):
        own = self.sem.get(eng)
        for b in reads:
            for tok in list(b.w.values()):
                self._maybe(eng, tok)
            if b.psum:
                for tok in list(b.r.values()):
                    if tok[0] is not own:
                        self._maybe(eng, tok)
        for b in writes:
            for tok in list((b.base if join else b.w).values()):
                self._maybe(eng, tok)
            for tok in list(b.r.values()):
                self._maybe(eng, tok)

    def _record(self, tok, reads, writes, join):
        name = tok[0].name
        for b in reads:
            b.r[name] = tok
        for b in writes:
            if join:
                b.w[name] = tok
            else:
                b.w = {name: tok}
                b.base = {name: tok}
            b.r = {}

    def op(self, eng, fn, reads=(), writes=(), join=False):
        reads = [b for b in reads if b is not None]
        self._deps(eng, reads, writes, join)
        ins = fn(self.E[eng])
        self.idx[eng] += 1
        if self.needed is None or (self.sem[eng].name, self.idx[eng]) in self.needed:
            self.cnt[eng] += 1
            ins.then_inc(self.sem[eng], 1)
        self.n_ins += 1
        self._record((self.sem[eng], self.cnt[eng]), reads, writes, join)

    def dma(self, q, out, in_, join=False, **kw):
        d = self.dq[q]
        i = d["rr"]
        d["rr"] = (i + 1) % len(d["sems"])
        sem = d["sems"][i]
        if d["cnt"][i] > 0:
            self._wait(q, (sem, d["cnt"][i]))
        reads, writes = [in_.buf], [out.buf]
        self._deps(q, reads, writes, join)
        ins = self.E[q].dma_start(out=out.ap, in_=in_.ap, **kw)
        d["cnt"][i] += 16
        ins.then_inc(sem, 16)
        self.n_ins += 1
        self._record((sem, d["cnt"][i]), reads, writes, join)

    def all_tokens(self):
        toks = [(self.sem[k], self.cnt[k]) for k in self.sem if self.cnt[k] > 0]
        for d in self.dq.values():
            toks += [(s, c) for s, c in zip(d["sems"], d["cnt"]) if c > 0]
        return toks

    def barrier(self, engines=("pe", "act", "dve", "pool", "sp")):
        toks = self.all_tokens()
        for e in engines:
            for tok in toks:
                if e in self.sem and tok[0] is self.sem[e]:
                    continue
                self._wait(e, tok)

    def finish(self):
        for tok in self.all_tokens():
            self._wait("sp", tok)


class MK:
    def __init__(self, debug_stop=None, dbg=False, needed=None):
        self.nc = nc = bass.Bass("TRN2", target_bir_lowering=False)
        self.S = Sched(nc, needed=needed)
        self.debug_stop = debug_stop
        self.dbg = dbg
        self.cast_rr = 0
        self.declare_io()

    def dram(self, name, shape, dt, kind="Internal"):
        if kind == "Internal" and self.dbg:
            kind = "ExternalOutput"
        return T(self.nc.dram_tensor(name, list(shape), dt, kind=kind).ap(), Buf(name))

    def uname(self, name):
        self.uid = getattr(self, "uid", 0) + 1
        return f"{name}_u{self.uid}"

    def sb(self, es, name, shape, dt):
        h = es.enter_context(self.nc.sbuf_tensor(self.uname(name), list(shape), dt))
        return T(h.ap(), Buf(name))

    def ps(self, es, name, shape=(128, 512), dt=F32):
        h = es.enter_context(self.nc.psum_tensor(self.uname(name), list(shape), dt))
        return T(h.ap(), Buf(name, psum=True))

    @staticmethod
    def _ce(eng):
        return eng

    def _rw(self, outs, ins):
        return [x.buf for x in ins if isinstance(x, T)], [x.buf for x in outs]

    def mm(self, out, lhsT, rhs, start=True, stop=True):
        r, w = self._rw([out], [lhsT, rhs])
        self.S.op("pe", lambda e: e.matmul(out.ap, lhsT.ap, rhs.ap, start=start, stop=stop), r, w)

    def tr(self, out, in_, ident):
        r, w = self._rw([out], [in_, ident])
        self.S.op("pe", lambda e: e.transpose(out.ap, in_.ap, ident.ap), r, w)

    def act(self, out, in_, func, bias=0.0, scale=1.0, accum=None, join=False):
        r, w = self._rw([out] + ([accum] if accum is not None else []), [in_, bias, scale])
        kw = {}
        if accum is not None:
            kw["accum_out"] = accum.ap
        self.S.op("act", lambda e: e.activation(out.ap, in_.ap, func, bias=_ap(bias), scale=_ap(scale), **kw), r, w, join)

    def tt(self, eng, out, a, b, op, join=False):
        eng = self._ce(eng)
        r, w = self._rw([out], [a, b])
        self.S.op(eng, lambda e: e.tensor_tensor(out.ap, a.ap, b.ap, op), r, w, join)

    def ts(self, eng, out, a, s1, op0, s2=None, op1=None, join=False):
        eng = self._ce(eng)
        r, w = self._rw([out], [a, s1, s2])
        if op1 is None:
            self.S.op(eng, lambda e: e.tensor_scalar(out.ap, a.ap, _ap(s1), None, op0), r, w, join)
        else:
            self.S.op(eng, lambda e: e.tensor_scalar(out.ap, a.ap, _ap(s1), _ap(s2), op0, op1), r, w, join)

    def stt(self, eng, out, a, s, b, op0, op1, join=False):
        eng = "dve"
        r, w = self._rw([out], [a, s, b])
        self.S.op(eng, lambda e: e.scalar_tensor_tensor(out.ap, a.ap, _ap(s), b.ap, op0, op1), r, w, join)

    def cp(self, eng, out, in_, join=False):
        eng = self._ce(eng)
        if eng == "act":
            return self.act(out, in_, AF.Copy, join=join)
        r, w = self._rw([out], [in_])
        self.S.op(eng, lambda e: e.tensor_copy(out.ap, in_.ap), r, w, join)

    def recip(self, out, in_, join=False):
        r, w = self._rw([out], [in_])
        self.S.op("dve", lambda e: e.reciprocal(out.ap, in_.ap), r, w, join)

    def memset(self, eng, out, val, join=False):
        eng = self._ce(eng)
        self.S.op(eng, lambda e: e.memset(out.ap, val), [], [out.buf], join)

    def scan(self, out, a, b, init):
        r, w = self._rw([out], [a, b, init])
        self.S.op("dve", lambda e: e.tensor_tensor_scan(out.ap, a.ap, b.ap, _ap(init), ALU.mult, ALU.add), r, w)

    def dma(self, out, in_, q="sp", join=False, **kw):
        self.S.dma(q, out, in_, join=join, **kw)

    def cast_eng(self):
        e = ("dve", "pool", "act")[self.cast_rr % 3]
        self.cast_rr += 1
        return e

    def declare_io(self):
        I = lambda n, s, dt=F32: self.dram(n, s, dt, "ExternalInput")
        O = lambda n, s: self.dram(n, s, F32, "ExternalOutput")
        self.xs = I("xs", [NS, D])
        self.xp = I("xp", [2 * NP, D])
        self.ckv = I("ckv", [DEPTH, PAST, 128])
        self.ckr = I("ckr", [DEPTH, PAST, 32])
        self.cdk = I("cdk", [DEPTH, PAST, 256])
        self.cdv = I("cdv", [DEPTH, PAST, 256])
        self.cst = I("cst", [DEPTH, 2, 256])
        self.cb = I("cb", [1, D])
        self.cctx = I("cctx", [1, D])
        self.w_ada = I("w_ada", [DEPTH, D, 6 * D])
        self.b_ada = I("b_ada", [DEPTH, 6 * D])
        self.norm1_g = I("norm1_g", [DEPTH, D])
        self.norm2_g = I("norm2_g", [DEPTH, D])
        self.w_in = I("w_in", [DEPTH, D, IN_COLS])
        self.mla_q_norm_g = I("mla_q_norm_g", [DEPTH, 192])
        self.mla_w_uq = I("mla_w_uq", [DEPTH, 192, 384])
        self.mla_kv_norm_g = I("mla_kv_norm_g", [DEPTH, 128])
        self.mla_w_ukv = I("mla_w_ukv", [DEPTH, 128, 512])
        self.lru_conv_w = I("lru_conv_w", [DEPTH, 4, 256])
        self.lru_conv_b = I("lru_conv_b", [DEPTH, 256])
        self.lru_w_r = I("lru_w_r", [DEPTH, 2, 4, 64, 64])
        self.lru_b_r = I("lru_b_r", [DEPTH, 2, 256])
        self.lru_w_i = I("lru_w_i", [DEPTH, 2, 4, 64, 64])
        self.lru_b_i = I("lru_b_i", [DEPTH, 2, 256])
        self.lru_lambda = I("lru_lambda", [DEPTH, 2, 256])
        self.pool_w = I("pool_w", [DEPTH, 4, 64, 64])
        self.pool_scale = I("pool_scale", [DEPTH, 256])
        self.diff_lambda = I("diff_lambda", [DEPTH, 128])
        self.diff_norm_g = I("diff_norm_g", [DEPTH, 64])
        self.w_out = I("w_out", [DEPTH, D, D])
        self.w_gu = I("w_gu", [DEPTH, D, 2 * FF])
        self.w_down = I("w_down", [DEPTH, FF, D])
        self.final_g = I("final_norm_g", [1, D])
        self.c_ident = I("c_ident", [128, 128])
        self.c_ropeC = I("c_ropeC", [128, NS])
        self.c_ropeS = I("c_ropeS", [128, NS])
        self.c_pinvS = I("c_pinvS", [4, NS])
        self.c_pinvP = I("c_pinvP", [4, NP])
        self.ys = O("ys", [NS, D])
        self.yp = O("yp", [2 * NP, D])
        self.o_ckv = O("o_ckv", [2, DEPTH, NP, 128])
        self.o_kr = O("o_kr", [2, DEPTH, NP, 32])
        self.o_dk = O("o_dk", [2, DEPTH, NP, 256])
        self.o_dv = O("o_dv", [2, DEPTH, NP, 256])
        self.o_st = O("o_st", [2, DEPTH, 2, 256])
        self.WinB = self.dram("WinB", [D, IN_COLS], BF)
        self.WoB = self.dram("WoB", [D, D], BF)
        self.WguB = self.dram("WguB", [D, 2 * FF], BF)
        self.WdB = self.dram("WdB", [FF, D], BF)
        self.seqs = []
        for name, n, nk in (("S", NS, NS + PAST), ("P0", NP, NP), ("P1", NP, NP)):
            q = dict(name=name, n=n, nk=nk, koff=nk - n, prompt=(name != "S"))
            q["pi"] = {"S": -1, "P0": 0, "P1": 1}[name]
            q["QTm"] = self.dram("QTm" + name, [4, 96, n], BF)
            q["KTm"] = self.dram("KTm" + name, [4, 96, nk], BF)
            q["Vm"] = self.dram("Vm" + name, [nk, 256], BF)
            q["DQ"] = self.dram("DQ" + name, [256, n], BF)
            q["DK"] = self.dram("DK" + name, [256, nk], BF)
            q["DV"] = self.dram("DV" + name, [nk, 256], BF)
            q["LXB"] = self.dram("LXB" + name, [256, n], F32)
            q["LGB"] = self.dram("LGB" + name, [256, n], F32)
            q["PU"] = self.dram("PU" + name, [256, n], F32)
            q["YT"] = self.dram("YT" + name, [D, n], BF)
            q["XA"] = self.dram("XA" + name, [n, D], F32)
            q["XB"] = self.dram("XB" + name, [n, D], F32)
            q["XC"] = self.dram("XC" + name, [n, D], F32)
            if name == "S":
                q["xin"], q["yout"] = self.xs, self.ys
            else:
                k = q["pi"]
                q["xin"] = T(self.xp.ap[k * NP:(k + 1) * NP, :], Buf())
                q["yout"] = T(self.yp.ap[k * NP:(k + 1) * NP, :], Buf())
            self.seqs.append(q)

    def load_cast(self, dst, src, rows, cols, stage, kcs=None):
        KC = rows // 128
        srcv = src.re("(k p) n -> p k n", p=128)
        maxe = stage[0].ap.shape[1]
        cw = min(cols, 512)
        kper = max(1, min(KC, maxe // cw))
        i = 0
        for c0 in range(0, cols, cw):
            c1 = min(cols, c0 + cw)
            for k0 in range(0, KC, kper):
                k1 = min(KC, k0 + kper)
                st = stage[i % len(stage)]
                i += 1
                n = (k1 - k0) * (c1 - c0)
                sv = T(st.ap[:, 0:n].rearrange("p (k n) -> p k n", k=k1 - k0), st.buf)
                self.dma(sv, srcv[:, k0:k1, c0:c1])
                self.cp(self.cast_eng(), dst[:, k0:k1, c0:c1], sv, join=True)

    def load_bf16(self, dst, src, rows, cols):
        KC = rows // 128
        srcv = src.re("(k p) n -> p k n", p=128)
        kper = max(1, min(KC, 4096 // cols))
        for i, k0 in enumerate(range(0, KC, kper)):
            k1 = min(KC, k0 + kper)
            self.dma(dst[:, k0:k1, :], srcv[:, k0:k1, :], join=(i > 0))

    def precast_jobs(self, L):
        jobs = []
        lst = [(self.w_out[L], self.WoB, D, D), (self.w_gu[L], self.WguB, D, 2 * FF), (self.w_down[L], self.WdB, FF, D)]
        if L + 1 < getattr(self, "nlayers", DEPTH):
            lst.append((self.w_in[L + 1], self.WinB, D, IN_COLS))
        for src, dst, rows, cols in lst:
            sv = src.re("(k p) n -> p k n", p=128)
            dv = dst.re("(k p) n -> p k n", p=128)
            KC = rows // 128
            for c0 in range(0, cols, 512):
                c1 = min(cols, c0 + 512)
                for k0 in range(0, KC, 4):
                    k1 = min(KC, k0 + 4)
                    jobs.append((sv[:, k0:k1, c0:c1], dv[:, k0:k1, c0:c1], k1 - k0, c1 - c0))
        self.pc_jobs = jobs
        self.pc_i = 0

    def precast_step(self, n=1):
        def views(job, i):
            src, dst, nk, nc_ = job
            f, b = self.pc_f[i % 2], self.pc_b[i % 2]
            fv = T(f.ap[:, 0:nk * nc_].rearrange("p (k n) -> p k n", k=nk), f.buf)
            bv = T(b.ap[:, 0:nk * nc_].rearrange("p (k n) -> p k n", k=nk), b.buf)
            return fv, bv

        for _ in range(n):
            cur = getattr(self, "pc_cur", None)
            if cur is None and not self.pc_jobs:
                return
            if self.pc_jobs:
                nxt = (self.pc_jobs.pop(0), self.pc_i)
                self.pc_i += 1
                fv, _ = views(*nxt)
                self.dma(fv, nxt[0][0], q="pool")
            else:
                nxt = None
            if cur is not None:
                fv, bv = views(*cur)
                self.cp("dve", bv, fv)
                self.dma(cur[0][1], bv, q="pool")
            self.pc_cur = nxt

    def load_cols(self, dst, src1d, n, q="sp"):
        self.dma(dst, T(src1d.ap.rearrange("(c p) -> p c", p=128), src1d.buf), q=q, allow_slow_non_contiguous=True)

    def build(self):
        nc = self.nc
        with ExitStack() as g:
            self.ident_f = self.sb(g, "ident_f", [128, 128], F32)
            self.ident_b = self.sb(g, "ident_b", [128, 128], BF)
            self.ones_b = self.sb(g, "ones_b", [128, 128], BF)
            self.ones_f = self.sb(g, "ones_f", [1, 128], F32)
            self.dma(self.ident_f, self.c_ident)
            self.cp("dve", self.ident_b, self.ident_f)
            self.memset("dve", self.ones_b, 1.0)
            self.memset("dve", self.ones_f, 1.0)
            self.modD = [[self.dram(f"modD{L}_{c}", [1, 6 * D], F32) for c in range(2)] for L in range(DEPTH)]
            self.fin_g = self.sb(g, "fin_g", [128, D], F32)
            self.dma(self.fin_g, T(self.final_g.ap[0:1, :].partition_broadcast(128), self.final_g.buf))
            for L in range(getattr(self, 'nlayers', DEPTH)):
                if L == 0:
                    self.phase0_mod(L)
                    self.S.barrier()
                self.phase1_inproj(L)
                self.S.barrier()
                if self.debug_stop == ("p1", L):
                    break
                with ExitStack() as lay:
                    self.pc_f = [self.sb(lay, f"pcf{i}", [128, 2048], F32) for i in range(2)]
                    self.pc_b = [self.sb(lay, f"pcb{i}", [128, 2048], BF) for i in range(2)]
                    self.Va = self.sb(lay, "Va", [128, 36, 4, 128], BF)
                    self.memset("pool", self.Va, 1.0)
                    self.precast_jobs(L)
                    self.phase2_mla(L)
                    self.S.barrier()
                    self.phase3_diff(L)
                    self.precast_step(10 ** 6)
                    self.S.barrier()
                self.phase4_lru(L)
                self.S.barrier()
                self.phase6_outproj(L)
                self.S.barrier()
                self.phase7_ffn(L)
                self.S.barrier()
            self.S.finish()
        return nc

    def phase0_mod(self, L):
        with ExitStack() as es:
            for _ in self.p0_gen(L, es):
                pass

    def p0_gen(self, L, es):
        CW = 256
        ccol = self.sb(es, "ccol", [128, 2, 8], F32)
        sil = self.sb(es, "sil", [128, 2, 8], BF)
        brow = [self.sb(es, f"brow{i}", [1, CW], F32) for i in range(2)]
        grow = [self.sb(es, f"grow{i}", [1, CW], F32) for i in range(2)]
        orow = [self.sb(es, f"orow{i}", [1, 2, CW], F32) for i in range(2)]
        wf = [self.sb(es, f"p0wf{i}", [128, 8, CW], F32) for i in range(2)]
        wb = [self.sb(es, f"p0wb{i}", [128, 8, CW], BF) for i in range(2)]
        pp = [self.ps(es, f"p0p{i}") for i in range(2)]
        self.dma(ccol[:, 0, :], T(self.cb.ap.rearrange("o (c p) -> p (o c)", p=128), self.cb.buf), allow_slow_non_contiguous=True)
        self.dma(ccol[:, 1, :], T(self.cctx.ap.rearrange("o (c p) -> p (o c)", p=128), self.cctx.buf), join=True, allow_slow_non_contiguous=True)
        self.act(sil, ccol, AF.Silu)
        wv = self.w_ada[L].re("(k p) n -> p k n", p=128)
        nch = 6 * D // CW
        gsrc = {1: self.norm1_g, 4: self.norm2_g}

        def load(j):
            self.dma(wf[j % 2], wv[:, :, j * CW:(j + 1) * CW])
            self.dma(brow[j % 2], self.b_ada[L:L + 1, j * CW:(j + 1) * CW])
            seg = (j * CW) // D
            if seg in gsrc:
                c0 = j * CW - seg * D
                self.dma(grow[j % 2], gsrc[seg][L:L + 1, c0:c0 + CW])

        load(0)
        yield
        for j in range(nch):
            if j + 1 < nch:
                load(j + 1)
            seg = (j * CW) // D
            self.cp("dve", wb[j % 2], wf[j % 2])
            p = pp[j % 2]
            o = orow[j % 2]
            for ci in range(2):
                for k in range(8):
                    self.mm(p[0:1, ci * CW:(ci + 1) * CW], sil[:, ci, k:k + 1], wb[j % 2][:, k, :], start=(k == 0), stop=(k == 7))
            for ci in range(2):
                self.tt("dve", o[:, ci, :], p[0:1, ci * CW:(ci + 1) * CW], brow[j % 2], ALU.add, join=(ci > 0))
                if seg in gsrc:
                    self.stt("dve", o[:, ci, :], o[:, ci, :], 1.0, grow[j % 2], ALU.add, ALU.mult)
            for ci in range(2):
                self.dma(self.modD[L][ci][0:1, j * CW:(j + 1) * CW], o[:, ci, :], q="sp", join=True)
            yield

    def mod_rows(self, es, L, segs, pref):
        out = {}
        for ci in range(2):
            for sg_ in segs:
                t = self.sb(es, f"{pref}m{ci}_{sg_}", [128, D], F32)
                src = self.modD[L][ci]
                self.dma(t, T(src.ap[0:1, sg_ * D:(sg_ + 1) * D].partition_broadcast(128), src.buf))
                out[(ci, sg_)] = t
        return out

    def norm_batch(self, xs, A, B, outs, ss, sd):
        n = len(xs)
        for i in range(n):
            self.act(outs[i], xs[i], AF.Square, accum=ss[:, i:i + 1], join=(i > 0))
        self.act(sd[:, 0:n], ss[:, 0:n], AF.Sqrt, bias=self.eps_col, scale=1.0 / D)
        self.recip(sd[:, 0:n], sd[:, 0:n])
        for i in range(n):
            self.stt("dve", xs[i], xs[i], sd[:, i:i + 1], A, ALU.mult, ALU.mult)
        for i in range(n):
            self.tt("dve", outs[i], xs[i], B, ALU.add)

    def norm_rows(self, es_tmp, xt, A, B, out, tmp, junk, ss, sd, eng2="pool"):
        self.act(junk, xt, AF.Square, accum=ss)
        self.act(sd, ss, AF.Sqrt, bias=self.eps_col, scale=1.0 / D)
        self.recip(sd, sd)
        if B is None:
            self.stt("dve", out, xt, sd, A, ALU.mult, ALU.mult)
        else:
            self.stt("dve", tmp, xt, sd, A, ALU.mult, ALU.mult)
            self.tt(eng2, out, tmp, B, ALU.add)

    def phase1_inproj(self, L):
        with ExitStack() as es:
            sb = lambda n, s, dt=F32: self.sb(es, n, s, dt)
            stage = [sb(f"stg{i}", [128, 2048]) for i in range(2)]
            Win = sb("Win", [128, 8, IN_COLS], BF)
            WinP = sb("WinP", [128, 8, 544], BF)
            Wq = sb("Wq", [128, 2, 4, 96], BF)
            WqP = sb("WqP", [128, 2, 4, 96], BF)
            Wk = sb("Wk", [128, 4, 96], BF)
            Wv = sb("Wv", [128, 4, 64], BF)
            Isel = sb("Isel", [32, 96], BF)
            gq = sb("gq", [128, 2])
            gkv = sb("gkv", [128, 1])
            self.eps_col = sb("eps_col", [128, 1])
            self.memset("dve", self.eps_col, EPS)
            mr = self.mod_rows(es, L, (0, 1), "p1")
            if L == 0:
                self.load_cast(Win, self.w_in[L], D, IN_COLS, stage)
            else:
                self.load_bf16(Win, self.WinB, D, IN_COLS)
            for (dst0, src0, nb) in ((0, O_DQ, 16), (512, O_KR, 1)):
                src = Win[:, :, src0:src0 + nb * 32].re("p k (b h j) -> p k b h j", h=2, j=16)
                dst = WinP[:, :, dst0:dst0 + nb * 32].re("p k (b h j) -> p k b h j", h=2, j=16)
                self.ts("dve", dst[:, :, :, 0, :], src[:, :, :, 1, :], -1.0, ALU.mult, join=True)
                self.cp("pool", dst[:, :, :, 1, :], src[:, :, :, 0, :], join=True)
            wq_st = sb("wq_st", [128, 2, 384])
            self.memset("pool", wq_st, 0.0)
            self.dma(wq_st[:, 0, :], self.mla_w_uq[L, 0:128, :], join=True)
            self.dma(wq_st[0:64, 1, :], self.mla_w_uq[L, 128:192, :], join=True)
            wq4 = wq_st.re("p k (h e) -> p k h e", e=96)
            self.cp("dve", Wq, wq4)
            self.memset("pool", WqP, 0.0)
            self.ts("dve", WqP[:, :, :, 64:80], wq4[:, :, :, 80:96], -1.0, ALU.mult)
            self.cp("dve", WqP[:, :, :, 80:96], wq4[:, :, :, 64:80], join=True)
            wkv_st = sb("wkv_st", [128, 512])
            self.dma(wkv_st, self.mla_w_ukv[L])
            wkv4 = wkv_st.re("p (h e) -> p h e", e=128)
            self.memset("pool", Wk, 0.0)
            self.cp("dve", Wk[:, :, 0:64], wkv4[:, :, 0:64])
            self.cp("dve", Wv, wkv4[:, :, 64:128])
            self.memset("pool", Isel, 0.0)
            self.cp("pool", Isel[:, 64:96], self.ident_b[0:32, 0:32])
            self.dma(gq[:, 0:1], T(self.mla_q_norm_g.ap[L, 0:128].rearrange("(p o) -> p o", o=1), self.mla_q_norm_g.buf))
            self.dma(gq[0:64, 1:2], T(self.mla_q_norm_g.ap[L, 128:192].rearrange("(p o) -> p o", o=1), self.mla_q_norm_g.buf), join=True)
            self.dma(gkv, T(self.mla_kv_norm_g.ap[L, :].rearrange("(p o) -> p o", o=1), self.mla_kv_norm_g.buf))
            xt = [sb(f"xt{i}", [128, D]) for i in range(4)]
            hn = [sb(f"hn{i}", [128, D], BF) for i in range(4)]
            ss4 = sb("ss4", [128, 4])
            sd4 = sb("sd4", [128, 4])
            hT = [sb(f"hT{i}", [128, 8, 512], BF) for i in range(2)]
            ropeC = [sb(f"ropeC{i}", [128, 512]) for i in range(2)]
            ropeS = [sb(f"ropeS{i}", [128, 512]) for i in range(2)]
            sq = sb("sq", [128, 2, 512], BF)
            sq2 = sb("sq2", [128, 512], BF)
            rst = sb("rst", [128, 512])
            rst2 = sb("rst2", [128, 512])
            cqn = sb("cqn", [128, 2, 512], BF)
            lat = sb("lat", [128, 512], BF)
            latf = sb("latf", [128, 512])
            krT = sb("krT", [32, 512], BF)
            krf = sb("krf", [32, 512])
            t1 = [sb(f"t1_{i}", [128, 512]) for i in range(2)]
            t2 = [sb(f"t2_{i}", [128, 512]) for i in range(2)]
            ob = [sb(f"ob{i}", [128, 512], BF) for i in range(4)]
            of = [sb(f"of{i}", [128, 512]) for i in range(4)]
            vb = [sb(f"vb{i}", [128, 256], BF) for i in range(2)]
            vf = [sb(f"vf{i}", [128, 256]) for i in range(2)]
            cst_f = sb("cst_f", [128, 4, 256])
            cst_b = sb("cst_b", [128, 4, 256], BF)
            pT2 = [self.ps(es, f"pT{i}") for i in range(2)]
            pT = pT2[0]
            pQ = [self.ps(es, f"pQ{i}") for i in range(2)]
            pC = self.ps(es, "pC")
            pR = [self.ps(es, f"pR{i}") for i in range(3)]
            rr = [0]
            obr = [0]
            tr_ = [0]

            def nps():
                p = pR[rr[0] % 3]
                rr[0] += 1
                return p

            def nob():
                o = ob[obr[0] % 4], of[obr[0] % 4]
                obr[0] += 1
                return o

            def ntt():
                t = t1[tr_[0] % 2], t2[tr_[0] % 2]
                tr_[0] += 1
                return t

            pTb = pT.bitcast(BF)
            blocks = []
            for q in self.seqs:
                bs = 512 if q["n"] >= 512 else q["n"]
                for t0 in range(0, q["n"], bs):
                    blocks.append((q, t0, bs))
            vrr = [0]

            def loads(bi):
                q, t0, nt = blocks[bi]
                for j in range(nt // 128):
                    self.dma(xt[j], q["xin_cur"][t0 + j * 128:t0 + (j + 1) * 128, :])
                if not q["prompt"]:
                    self.dma(ropeC[bi % 2], self.c_ropeC[:, t0:t0 + nt])
                    self.dma(ropeS[bi % 2], self.c_ropeS[:, t0:t0 + nt])

            def chain(bi):
                q, t0, nt = blocks[bi]
                ci = 1 if q["prompt"] else 0
                nj_ = nt // 128
                self.norm_batch(xt[0:nj_], mr[(ci, 1)], mr[(ci, 0)], hn[0:nj_], ss4, sd4)

            pTbs = [p.bitcast(BF) for p in pT2]

            def transposes(bi):
                q, t0, nt = blocks[bi]
                h_ = hT[bi % 2]
                for j in range(nt // 128):
                    ptb = pTbs[j % 2]
                    for k in range(8):
                        self.tr(ptb[:, k * 128:(k + 1) * 128], hn[j][:, k * 128:(k + 1) * 128], self.ident_b)
                    self.cp("act", h_[:, :, j * 128:(j + 1) * 128], ptb.re("p (k t) -> p k t", k=8), join=(j > 0))

            loads(0)
            chain(0)
            transposes(0)
            for bi, (q, t0, nt) in enumerate(blocks):
                prompt = q["prompt"]
                koff = q["koff"]
                tk = slice(koff + t0, koff + t0 + nt)
                tq = slice(t0, t0 + nt)
                hTb = hT[bi % 2]
                rC, rS = ropeC[bi % 2], ropeS[bi % 2]
                more = bi + 1 < len(blocks)
                if more:
                    loads(bi + 1)

                def proj(pout, m, c0, W=Win):
                    for k in range(8):
                        self.mm(pout[0:m, 0:nt], W[:, k, c0:c0 + m], hTb[:, k, 0:nt], start=(k == 0), stop=(k == 7))

                proj(pQ[0], 128, O_CQ)
                proj(pQ[1], 64, O_CQ + 128)
                proj(pC, 128, O_CKV)
                pk = nps()
                proj(pk, 32, O_KR)
                self.act(sq[:, 0, 0:nt], pQ[0][:, 0:nt], AF.Square)
                self.act(sq[0:64, 1, 0:nt], pQ[1][0:64, 0:nt], AF.Square, join=True)
                self.act(sq2[:, 0:nt], pC[:, 0:nt], AF.Square)
                if not prompt:
                    pp = nps()
                    proj(pp, 32, 512, WinP)
                    a1, a2 = ntt()
                    self.tt("dve", a1[0:32, 0:nt], pk[0:32, 0:nt], rC[0:32, 0:nt], ALU.mult)
                    self.tt("dve", a2[0:32, 0:nt], pp[0:32, 0:nt], rS[0:32, 0:nt], ALU.mult)
                    self.tt("dve", krT[:, 0:nt], a1[0:32, 0:nt], a2[0:32, 0:nt], ALU.add)
                else:
                    self.cp("act", krf[:, 0:nt], pk[0:32, 0:nt])
                    self.cp("pool", krT[:, 0:nt], krf[:, 0:nt])
                if more:
                    chain(bi + 1)
                for (c0, dst) in ((O_XB, "LXB"), (O_GB, "LGB"), (O_PU, "PU")):
                    for cc in range(2):
                        p = nps()
                        proj(p, 128, c0 + cc * 128)
                        _, o = nob()
                        self.cp("act", o[:, 0:nt], p[:, 0:nt])
                        self.dma(q[dst][cc * 128:(cc + 1) * 128, tq], o[:, 0:nt], q="sp", join=True)
                pSS = nps()
                self.mm(pSS[:, 0:nt], self.ones_b[:, :], sq[:, 0, 0:nt], start=True, stop=False)
                self.mm(pSS[:, 0:nt], self.ones_b[0:64, :], sq[0:64, 1, 0:nt], start=False, stop=True)
                self.act(rst[:, 0:nt], pSS[:, 0:nt], AF.Sqrt, bias=self.eps_col, scale=1.0 / 192)
                pSS = nps()
                self.mm(pSS[:, 0:nt], self.ones_b[:, :], sq2[:, 0:nt])
                self.act(rst2[:, 0:nt], pSS[:, 0:nt], AF.Sqrt, bias=self.eps_col, scale=1.0 / 128)
                self.recip(rst[:, 0:nt], rst[:, 0:nt])
                self.recip(rst2[:, 0:nt], rst2[:, 0:nt])
                self.stt("dve", cqn[:, 0, 0:nt], pQ[0][:, 0:nt], gq[:, 0:1], rst[:, 0:nt], ALU.mult, ALU.mult)
                self.stt("dve", cqn[0:64, 1, 0:nt], pQ[1][0:64, 0:nt], gq[0:64, 1:2], rst[0:64, 0:nt], ALU.mult, ALU.mult, join=True)
                if prompt:
                    self.stt("dve", latf[:, 0:nt], pC[:, 0:nt], gkv[:, 0:1], rst2[:, 0:nt], ALU.mult, ALU.mult)
                    self.cp("pool", lat[:, 0:nt], latf[:, 0:nt])
                else:
                    self.stt("dve", lat[:, 0:nt], pC[:, 0:nt], gkv[:, 0:1], rst2[:, 0:nt], ALU.mult, ALU.mult)
                for (c0, dstn, sl, pc0) in ((O_DQ, "DQ", tq, 0), (O_DK, "DK", tk, 256)):
                    for cc in range(2):
                        pm = nps()
                        proj(pm, 128, c0 + cc * 128)
                        o, ofp = nob()
                        if not prompt:
                            pp = nps()
                            proj(pp, 128, pc0 + cc * 128, WinP)
                            a1, a2 = ntt()
                            self.tt("dve", a1[:, 0:nt], pm[:, 0:nt], rC[:, 0:nt], ALU.mult)
                            self.tt("dve", a2[:, 0:nt], pp[:, 0:nt], rS[:, 0:nt], ALU.mult)
                            self.tt("dve", o[:, 0:nt], a1[:, 0:nt], a2[:, 0:nt], ALU.add)
                        else:
                            if dstn == "DK":
                                self.cp("act", ofp[:, 0:nt], pm[:, 0:nt])
                                self.cp("pool", o[:, 0:nt], ofp[:, 0:nt])
                                for j in range(nt // 128):
                                    pt = nps()
                                    self.tr(pt[:, 0:128], ofp[:, j * 128:(j + 1) * 128], self.ident_f)
                                    v = vf[vrr[0] % 2]
                                    vrr[0] += 1
                                    self.cp("act", v[:, 0:128], pt[:, 0:128])
                                    self.dma(self.o_dk[q["pi"], L, j * 128:(j + 1) * 128, cc * 128:(cc + 1) * 128], v[:, 0:128], q="sp", join=True)
                            else:
                                self.cp("act", o[:, 0:nt], pm[:, 0:nt])
                        self.dma(q[dstn][cc * 128:(cc + 1) * 128, sl], o[:, 0:nt], q="sp", join=True)
                for hh in range(4):
                    pm = nps()
                    self.mm(pm[0:96, 0:nt], Wq[:, 0, hh, :], cqn[:, 0, 0:nt], start=True, stop=False)
                    self.mm(pm[0:96, 0:nt], Wq[0:64, 1, hh, :], cqn[0:64, 1, 0:nt], start=False, stop=True)
                    o, _ = nob()
                    if not prompt:
                        pp = nps()
                        self.mm(pp[0:96, 0:nt], WqP[:, 0, hh, :], cqn[:, 0, 0:nt], start=True, stop=False)
                        self.mm(pp[0:96, 0:nt], WqP[0:64, 1, hh, :], cqn[0:64, 1, 0:nt], start=False, stop=True)
                        a1, a2 = ntt()
                        self.tt("dve", a1[64:96, 0:nt], pm[64:96, 0:nt], rC[64:96, 0:nt], ALU.mult)
                        self.tt("dve", a2[64:96, 0:nt], pp[64:96, 0:nt], rS[64:96, 0:nt], ALU.mult)
                        self.tt("dve", o[64:96, 0:nt], a1[64:96, 0:nt], a2[64:96, 0:nt], ALU.add)
                        self.cp("act", o[0:64, 0:nt], pm[0:64, 0:nt], join=True)
                    else:
                        self.cp("act", o[0:96, 0:nt], pm[0:96, 0:nt])
                    self.dma(q["QTm"][hh, :, tq], o[0:96, 0:nt], q="sp", join=True)
                self.emit_kv(q, lat, krT, nt, tk, Wk, Wv, Isel, nps, nob, vb, vrr)
                if prompt:
                    pi = q["pi"]
                    for j in range(nt // 128):
                        pt = nps()
                        self.tr(pt[:, 0:128], latf[:, j * 128:(j + 1) * 128], self.ident_f)
                        self.tr(pt[:, 128:160], krf[:, j * 128:(j + 1) * 128], self.ident_f[0:32, 0:32])
                        _, o = nob()
                        self.cp("act", o[:, 0:160], pt[:, 0:160])
                        self.dma(self.o_ckv[pi, L, j * 128:(j + 1) * 128, :], o[:, 0:128], q="sp", join=True)
                        self.dma(self.o_kr[pi, L, j * 128:(j + 1) * 128, :], o[:, 128:160], q="sp", join=True)
                for j in range(nt // 128):
                    p = nps()
                    for k in range(8):
                        self.mm(p[:, 0:256], hTb[:, k, j * 128:(j + 1) * 128], Win[:, k, O_DV:O_DV + 256], start=(k == 0), stop=(k == 7))
                    v = vb[vrr[0] % 2]
                    vrr[0] += 1
                    self.cp("act", v, p[:, 0:256])
                    self.dma(q["DV"][koff + t0 + j * 128:koff + t0 + (j + 1) * 128, :], v, q="sp", join=True)
                    if prompt:
                        v2 = vf[vrr[0] % 2]
                        self.cp("dve", v2, p[:, 0:256])
                        self.dma(self.o_dv[q["pi"], L, j * 128:(j + 1) * 128, :], v2, q="sp", join=True)
                if more:
                    transposes(bi + 1)
            q = self.seqs[0]
            self.dma(cst_f[:, :, 0:128], T(self.ckv.ap[L].rearrange("(j p) e -> p j e", p=128), self.ckv.buf))
            self.dma(cst_f[:, :, 128:160], T(self.ckr.ap[L].rearrange("(j p) e -> p j e", p=128), self.ckr.buf), join=True)
            self.cp("dve", cst_b[:, :, 0:160], cst_f[:, :, 0:160])
            for j in range(4):
                self.tr(pTb[:, j * 128:(j + 1) * 128], cst_b[:, j, 0:128], self.ident_b)
                self.tr(pTb[0:32, 512 + j * 128:512 + (j + 1) * 128], cst_b[:, j, 128:160], self.ident_b)
            self.cp("act", lat, pTb[:, 0:512])
            self.cp("act", krT, pTb[0:32, 512:1024])
            self.emit_kv(q, lat, krT, 512, slice(0, 512), Wk, Wv, Isel, nps, nob, vb, vrr)
            self.dma(cst_f, T(self.cdk.ap[L].rearrange("(j p) e -> p j e", p=128), self.cdk.buf))
            self.cp("dve", cst_b, cst_f)
            for cc in range(2):
                for j in range(4):
                    self.tr(pTb[:, j * 128:(j + 1) * 128], cst_b[:, j, cc * 128:(cc + 1) * 128], self.ident_b)
                o, _ = nob()
                self.cp("act", o, pTb[:, 0:512])
                self.dma(q["DK"][cc * 128:(cc + 1) * 128, 0:512], o, q="sp", join=True)
            self.dma(cst_f, T(self.cdv.ap[L].rearrange("(j p) e -> p j e", p=128), self.cdv.buf))
            self.cp("dve", cst_b, cst_f)
            self.dma(T(q["DV"].ap[0:512, :].rearrange("(j p) e -> p j e", p=128), q["DV"].buf), cst_b, q="sp", join=True)

    def emit_kv(self, q, lat, krT, nt, tk, Wk, Wv, Isel, nps, nob, vb, vrr):
        for hh in range(4):
            p = nps()
            self.mm(p[0:96, 0:nt], Wk[:, hh, :], lat[:, 0:nt], start=True, stop=False)
            self.mm(p[0:96, 0:nt], Isel[:, :], krT[:, 0:nt], start=False, stop=True)
            o, _ = nob()
            self.cp("act", o[0:96, 0:nt], p[0:96, 0:nt])
            self.dma(q["KTm"][hh, :, tk], o[0:96, 0:nt], q="sp", join=True)
        for j in range(nt // 128):
            p = nps()
            self.mm(p[:, 0:256], lat[:, j * 128:(j + 1) * 128], Wv.re("p h e -> p (h e)"))
            v = vb[vrr[0] % 2]
            vrr[0] += 1
            self.cp("act", v, p[:, 0:256])
            self.dma(q["Vm"][tk.start + j * 128:tk.start + (j + 1) * 128, :], v, q="sp", join=True)

    def attn_begin(self, pSs, PTs):
        self.at = dict(pSs=pSs, PTs=PTs, st=0, pt=0, pend=[], later=[])

    def attn_tick(self, force=False):
        a = self.at
        for it in a["later"]:
            it[0] -= 1
        while a["later"] and (force or a["later"][0][0] <= 0):
            a["later"].pop(0)[1]()

    def attn_flush(self, keep=0):
        a = self.at
        while len(a["pend"]) > keep:
            kc0, PT, pO, qbs, Vaug, nkc, cb = a["pend"].pop(0)
            for u in range(2):
                kc = kc0 + u
                self.mm(pO[:, 0:qbs], Vaug(kc), PT[:, u * qbs:(u + 1) * qbs], start=(kc == 0), stop=(kc == nkc - 1))
            if kc0 + 2 >= nkc and cb is not None:
                f = cb()
                if f is not None:
                    a["later"].append([8, f])
        if keep == 0:
            self.attn_tick(force=True)

    def attn_block(self, KT, QT, Vaug, kdim, q0, qbs, nk, scale, pO, cb):
        a = self.at
        nkc = nk // 128
        for kc0 in range(0, nkc, 2):
            pS = a["pSs"][a["st"] % len(a["pSs"])]
            PT = a["PTs"][a["pt"] % len(a["PTs"])]
            a["st"] += 1
            a["pt"] += 1
            for u in range(2):
                kc = kc0 + u
                self.mm(pS[:, u * qbs:(u + 1) * qbs], KT[0:kdim, kc * 128:(kc + 1) * 128], QT[0:kdim, q0:q0 + qbs])
            self.act(PT[:, 0:2 * qbs], pS[:, 0:2 * qbs], AF.Exp, scale=scale)
            a["pend"].append((kc0, PT, pO, qbs, Vaug, nkc, cb))
            self.attn_flush(keep=2)
            self.attn_tick()

    def phase2_mla(self, L):
        with ExitStack() as es:
            sb = lambda n, s, dt=F32: self.sb(es, n, s, dt)
            KT = [sb(f"KT{i}", [96, NS + PAST], BF) for i in range(2)]
            QT = [sb(f"QT{i}", [96, NS], BF) for i in range(2)]
            Va = self.Va
            PTs = [sb(f"PT{i}", [128, 1024], BF) for i in range(4)]
            rec = [sb(f"rec{i}", [64, 512]) for i in range(2)]
            yo = [sb(f"yo{i}", [64, 512], BF) for i in range(2)]
            pSs = [self.ps(es, f"pS{i}", [128, 1024]) for i in range(3)]
            pOs = [self.ps(es, f"pO{i}") for i in range(2)]
            cnt = 0
            oi = 0
            self.attn_begin(pSs, PTs)
            pg = self.pool_gen(L, es)
            scale = 1.0 / math.sqrt(96.0)
            for q in self.seqs:
                nq, nk = q["n"], q["nk"]
                nkc = nk // 128
                self.attn_flush()
                for h_ in range(4):
                    vsrc = T(q["Vm"].ap[:, h_ * 64:(h_ + 1) * 64].rearrange("(c p) e -> p c e", p=128), q["Vm"].buf)
                    for c0_ in range(0, nkc, 8):
                        c1_ = min(nkc, c0_ + 8)
                        self.dma(Va[:, c0_:c1_, h_, 0:64], vsrc[:, c0_:c1_, :], join=(h_ > 0 or c0_ > 0))
                for hh in range(4):
                    kt, qt = KT[cnt % 2], QT[cnt % 2]
                    cnt += 1
                    self.dma(kt[:, 0:nk], q["KTm"][hh])
                    self.dma(qt[:, 0:nq], q["QTm"][hh])
                    qbs = 512 if nq >= 512 else nq
                    for q0 in range(0, nq, qbs):
                        pO = pOs[oi % 2]
                        r, y = rec[oi % 2], yo[oi % 2]
                        oi += 1
                        self.precast_step()
                        next(pg, None)
                        next(pg, None)

                        def epi(pO=pO, r=r, y=y, q=q, hh=hh, q0=q0, qbs=qbs):
                            self.recip(r[:, 0:qbs], pO[64:128, 0:qbs])
                            self.tt("dve", y[:, 0:qbs], pO[0:64, 0:qbs], r[:, 0:qbs], ALU.mult)
                            self.dma(q["YT"][hh * 64:(hh + 1) * 64, q0:q0 + qbs], y[:, 0:qbs], q="pool")

                        self.attn_block(kt, qt, lambda kc, hh=hh: Va[:, kc, hh, :], 96, q0, qbs, nk, scale, pO, epi)
            self.attn_flush()
            for _ in pg:
                pass

    def phase3_diff(self, L):
        lam_init = 0.8 - 0.6 * math.exp(-0.3 * L)
        with ExitStack() as es:
            sb = lambda n, s, dt=F32: self.sb(es, n, s, dt)
            KT = [sb(f"dKT{i}", [128, NS + PAST], BF) for i in range(2)]
            Qz = [[sb(f"dQz{b}_{i}", [128, NS], BF) for i in range(4)] for b in range(2)]
            Va = self.Va
            PTs = [sb(f"dPT{i}", [128, 1024], BF) for i in range(4)]
            rec = sb("drec", [64, 512])
            o0 = sb("do0", [64, 512])
            o1 = sb("do1", [64, 512])
            osq = [sb(f"dosq{i}", [64, 512], BF) for i in range(2)]
            odq = [sb(f"dodq{i}", [64, 512]) for i in range(2)]
            gcol2 = sb("dgcol2", [64, 1])
            rs = sb("drs", [64, 512])
            yo = [sb(f"dyo{i}", [64, 512], BF) for i in range(2)]
            lv = sb("dlv", [128, 4, 32])
            lp = sb("dlp", [128, 2, 32])
            ls = sb("dls", [128, 2])
            nlam = sb("dnlam", [128, 1])
            gcol = sb("dgcol", [64, 1])
            epsc = sb("depsc", [128, 1])
            pSs = [self.ps(es, f"dpS{i}", [128, 1024]) for i in range(3)]
            pOs = [self.ps(es, f"dpO{i}") for i in range(2)]
            for b in range(2):
                for i in range(4):
                    self.memset("pool" if i % 2 else "dve", Qz[b][i], 0.0)
            self.memset("pool", epsc, EPS)
            self.dma(lv.re("p a b -> p (a b)"), T(self.diff_lambda.ap[L:L + 1, :].partition_broadcast(128), self.diff_lambda.buf))
            lv4 = lv.re("p (a b) e -> p a b e", b=2)
            self.tt("dve", lp, lv4[:, :, 0, :], lv4[:, :, 1, :], ALU.mult)
            self.S.op("dve", lambda e: e.tensor_reduce(ls.ap, lp.ap, AX.X, ALU.add), [lp.buf], [ls.buf])
            self.act(ls, ls, AF.Exp)
            self.tt("dve", nlam, ls[:, 1:2], ls[:, 0:1], ALU.subtract)
            self.ts("dve", nlam, nlam, -lam_init, ALU.add)
            self.dma(gcol, T(self.diff_norm_g.ap[L, :].rearrange("(p o) -> p o", o=1), self.diff_norm_g.buf))
            self.ts("dve", gcol, gcol, 1.0 - lam_init, ALU.mult)
            self.ts("dve", gcol2, gcol, 0.5, ALU.mult)
            cnt = 0
            oi = 0
            yi = 0
            self.attn_begin(pSs, PTs)
            scale = 1.0 / math.sqrt(32.0)
            for q in self.seqs:
                nq, nk = q["n"], q["nk"]
                nkc = nk // 128
                self.attn_flush()
                for h_ in range(4):
                    vsrc = T(q["DV"].ap[:, h_ * 64:(h_ + 1) * 64].rearrange("(c p) e -> p c e", p=128), q["DV"].buf)
                    for c0_ in range(0, nkc, 8):
                        c1_ = min(nkc, c0_ + 8)
                        self.dma(Va[:, c0_:c1_, h_, 0:64], vsrc[:, c0_:c1_, :], join=(h_ > 0 or c0_ > 0))
                for hp in range(2):
                    kt, qz = KT[cnt % 2], Qz[cnt % 2]
                    cnt += 1
                    self.dma(kt[:, 0:nk], q["DK"][hp * 128:(hp + 1) * 128, :])
                    for i in range(4):
                        self.dma(qz[i][32 * i:32 * i + 32, 0:nq], q["DQ"][hp * 128 + 32 * i:hp * 128 + 32 * i + 32, :], join=True)
                    qbs = 512 if nq >= 512 else nq
                    for hl in range(2):
                        hh = 2 * hp + hl
                        for q0 in range(0, nq, qbs):
                            self.precast_step()
                            pA, pB = pOs[oi % 2], pOs[(oi + 1) % 2]
                            oi += 2
                            y = yo[yi % 2]
                            yi += 1

                            def epi0(pA=pA, qbs=qbs):
                                self.recip(rec[:, 0:qbs], pA[64:128, 0:qbs])
                                self.tt("dve", o0[:, 0:qbs], pA[0:64, 0:qbs], rec[:, 0:qbs], ALU.mult)

                            def epi1(pB=pB, qbs=qbs, y=y, q=q, hh=hh, q0=q0, yi_=yi):
                                self.attn_tick(force=True)
                                self.recip(rec[:, 0:qbs], pB[64:128, 0:qbs])
                                self.tt("dve", o1[:, 0:qbs], pB[0:64, 0:qbs], rec[:, 0:qbs], ALU.mult)
                                self.stt("dve", o0[:, 0:qbs], o1[:, 0:qbs], nlam[0:64, 0:1], o0[:, 0:qbs], ALU.mult, ALU.add)
                                od, sq_ = odq[yi_ % 2], osq[yi_ % 2]
                                self.tt("pool", od[:, 0:qbs], o0[:, 0:qbs], o0[:, 0:qbs], ALU.add)
                                self.tt("pool", sq_[:, 0:qbs], o0[:, 0:qbs], o0[:, 0:qbs], ALU.mult)

                                def epi2():
                                    a = self.at
                                    pN = a["pSs"][a["st"] % len(a["pSs"])]
                                    a["st"] += 1
                                    self.mm(pN[0:64, 0:qbs], self.ones_b[0:64, 0:64], sq_[:, 0:qbs])
                                    self.act(rs[:, 0:qbs], pN[0:64, 0:qbs], AF.Ln, bias=epsc[0:64, :], scale=1.0 / 64)
                                    self.act(rs[:, 0:qbs], rs[:, 0:qbs], AF.Exp, scale=-0.5)
                                    self.stt("dve", y[:, 0:qbs], od[:, 0:qbs], gcol2[:, 0:1], rs[:, 0:qbs], ALU.mult, ALU.mult)
                                    self.dma(q["YT"][768 + hh * 64:768 + (hh + 1) * 64, q0:q0 + qbs], y[:, 0:qbs], q="pool")

                                return epi2

                            for c, (pO, cb) in enumerate(((pA, epi0), (pB, epi1))):
                                self.attn_block(kt, qz[2 * hl + c], lambda kc, hh=hh: Va[:, kc, hh, :], 128, q0, qbs, nk, scale, pO, cb)


            self.attn_flush()

    def phase4_lru(self, L):
        with ExitStack() as es:
            sb = lambda n, s, dt=F32: self.sb(es, n, s, dt)
            W = NS + 3
            X = sb("lX", [128, W])
            XC = sb("lXC", [128, NS])
            XCb = sb("lXCb", [128, NS], BF)
            A0 = sb("lA0", [128, NS])
            B0 = sb("lB0", [128, NS])
            S0 = sb("lS0", [128, NS])
            B1 = sb("lB1", [128, NS])
            S1 = sb("lS1", [128, NS])
            H0 = sb("lH0", [128, NS])
            Yb = sb("lYb", [128, NS], BF)
            wst = sb("lwst", [128, 8, 128])
            Wg = sb("lWg", [128, 8, 128], BF)
            cw = sb("lcw", [128, 4, 2])
            cbias = sb("lcb", [128, 2])
            gb = sb("lgb", [128, 2, 2, 2])
            lam = sb("llam", [128, 2, 2])
            cA = sb("lcA", [128, 2, 2])
            h0 = sb("lh0", [128, 2, 2])
            fin = sb("lfin", [128, 2])
            pG = [self.ps(es, f"lpG{i}") for i in range(4)]
            self.memset("pool", wst, 0.0)
            for gi, wsrc in enumerate((self.lru_w_r, self.lru_w_i)):
                for d in range(2):
                    for g4 in range(4):
                        cc, hf = g4 // 2, g4 % 2
                        idx = (gi * 2 + d) * 2 + cc
                        self.dma(wst[hf * 64:(hf + 1) * 64, idx, hf * 64:(hf + 1) * 64], wsrc[L, d, g4], join=True)
            self.cp("dve", Wg, wst)
            self.dma(cw, T(self.lru_conv_w.ap[L].rearrange("j (c p) -> p j c", p=128), self.lru_conv_w.buf), allow_slow_non_contiguous=True)
            self.load_cols(cbias, self.lru_conv_b[L], 2)
            self.dma(gb[:, 0], T(self.lru_b_r.ap[L].rearrange("d (c p) -> p d c", p=128), self.lru_b_r.buf), allow_slow_non_contiguous=True)
            self.dma(gb[:, 1], T(self.lru_b_i.ap[L].rearrange("d (c p) -> p d c", p=128), self.lru_b_i.buf), join=True, allow_slow_non_contiguous=True)
            self.dma(lam, T(self.lru_lambda.ap[L].rearrange("d (c p) -> p d c", p=128), self.lru_lambda.buf), allow_slow_non_contiguous=True)
            self.dma(h0, T(self.cst.ap[L].rearrange("d (c p) -> p d c", p=128), self.cst.buf), allow_slow_non_contiguous=True)
            self.act(cA, lam, AF.Exp, scale=-1.0)
            self.act(cA, cA, AF.Ln, bias=1.0)
            self.ts("dve", cA, cA, -LRU_C, ALU.mult)
            nl = getattr(self, "nlayers", DEPTH)
            p0 = self.p0_gen(L + 1, es) if L + 1 < nl else iter(())
            bufs = {"S": dict(X=X, XC=XC, XCb=XCb, A0=A0, B0=B0, S0=S0, B1=B1, S1=S1, H0=H0, Yb=Yb, fin=fin)}
            for nm in ("P0", "P1"):
                d_ = dict(X=sb("lX" + nm, [128, NP + 3]), XCb=sb("lXCb" + nm, [128, NP], BF), Yb=sb("lYb" + nm, [128, NP], BF),
                          fin=sb("lfin" + nm, [128, 2]))
                for k_ in ("XC", "A0", "B0", "S0", "B1", "S1", "H0"):
                    d_[k_] = sb("l" + k_ + nm, [128, NP])
                bufs[nm] = d_
            gctr = [0]

            def chain(q, cc):
                n = q["n"]
                bs = 512 if n >= 512 else n
                b_ = bufs[q["name"]]
                X, XC, XCb, A0, B0, S0, B1, S1, H0, Yb, fin = (b_[k_] for k_ in ("X", "XC", "XCb", "A0", "B0", "S0", "B1", "S1", "H0", "Yb", "fin"))
                A1 = T(X.ap[:, 0:n], X.buf)
                Ad, Bd, Sd = (A0, A1), (B0, B1), (S0, S1)
                H1, G, G2 = A0, S0, S1
                self.memset("pool", X[:, 0:1], 0.0)
                self.memset("pool", X[:, n + 1:n + 3], 0.0, join=True)
                self.dma(X[:, 1:n + 1], q["LXB"][cc * 128:(cc + 1) * 128, :], join=True)
                yield
                self.ts("dve", XC[:, 0:n], X[:, 0:n], cw[:, 0, cc:cc + 1], ALU.mult, cbias[:, cc:cc + 1], ALU.add)
                for j in range(1, 4):
                    self.stt("dve", XC[:, 0:n], X[:, j:j + n], cw[:, j, cc:cc + 1], XC[:, 0:n], ALU.mult, ALU.add)
                self.cp("act", XCb[:, 0:n], XC[:, 0:n])
                yield
                for bi, b0 in enumerate(range(0, n, bs)):
                    for d in range(2):
                        for gi, dst in ((0, Ad[d]), (1, Bd[d])):
                            idx = (gi * 2 + d) * 2 + cc
                            p = pG[gctr[0] % 4]
                            gctr[0] += 1
                            self.mm(p[:, 0:bs], Wg[:, idx, :], XCb[:, b0:b0 + bs])
                            self.act(dst[:, b0:b0 + bs], p[:, 0:bs], AF.Sigmoid, bias=gb[:, gi, d, cc:cc + 1], join=(bi > 0))
                    if bi % 2 == 1:
                        yield
                yield
                for d in range(2):
                    self.act(Ad[d][:, 0:n], Ad[d][:, 0:n], AF.Exp, scale=cA[:, d, cc:cc + 1])
                yield
                for d in range(2):
                    self.act(Sd[d][:, 0:n], Ad[d][:, 0:n], AF.Square)
                    self.tt("dve", Bd[d][:, 0:n], Bd[d][:, 0:n], XC[:, 0:n], ALU.mult)
                yield
                for d in range(2):
                    self.act(Sd[d][:, 0:n], Sd[d][:, 0:n], AF.Ln, bias=1.0, scale=-1.0)
                yield
                for d in range(2):
                    self.act(Sd[d][:, 0:n], Sd[d][:, 0:n], AF.Exp, scale=0.5)
                yield
                for d in range(2):
                    self.tt("dve", Bd[d][:, 0:n], Bd[d][:, 0:n], Sd[d][:, 0:n], ALU.mult)
                yield
                self.dma(G[:, 0:n], q["LGB"][cc * 128:(cc + 1) * 128, :])
                init0 = 0.0 if q["prompt"] else h0[:, 0, cc:cc + 1]
                init1 = 0.0 if q["prompt"] else h0[:, 1, cc:cc + 1]
                self.scan(H0[:, 0:n], A0[:, 0:n], B0[:, 0:n], init0)
                yield
                self.act(G2[:, 0:n], G[:, 0:n], AF.Square)
                self.act(G2[:, 0:n], G2[:, 0:n], AF.Identity, bias=1.0, scale=0.044715)
                self.tt("dve", G2[:, 0:n], G2[:, 0:n], G[:, 0:n], ALU.mult)
                yield
                self.act(G2[:, 0:n], G2[:, 0:n], AF.Sigmoid, scale=1.5957691216057308)
                rv = lambda t: T(AP(t.ap.tensor, t.ap.offset + n - 1, [list(t.ap.ap[0]), [-1, n]]), t.buf)
                self.scan(rv(H1), rv(A1), rv(B1), init1)
                yield
                self.tt("dve", G[:, 0:n], G[:, 0:n], G2[:, 0:n], ALU.mult)
                if q["prompt"]:
                    self.cp("dve", fin[:, 0:1], H0[:, n - 1:n])
                    self.cp("dve", fin[:, 1:2], H1[:, 0:1], join=True)
                    dst = T(self.o_st.ap[q["pi"], L, :, cc * 128:(cc + 1) * 128].rearrange("d p -> p d"), self.o_st.buf)
                    self.dma(dst, fin, q="pool", join=True, allow_slow_non_contiguous=True)
                yield
                self.tt("dve", H0[:, 0:n], H0[:, 0:n], H1[:, 0:n], ALU.add)
                self.tt("dve", Yb[:, 0:n], H0[:, 0:n], G[:, 0:n], ALU.mult)
                self.dma(q["YT"][256 + cc * 128:256 + (cc + 1) * 128, :], Yb[:, 0:n], q="pool", join=True)

            for cc in range(2):
                gens = [chain(q, cc) for q in self.seqs]
                while gens:
                    for g_ in list(gens):
                        try:
                            next(g_)
                        except StopIteration:
                            gens.remove(g_)
                    next(p0, None)
            for _ in p0:
                pass

    def pool_gen(self, L, es):
        PAD = 16
        sb = lambda n, s, dt=F32: self.sb(es, n, s, dt)
        W = NS + 2 * PAD
        X = sb("pX", [128, W])
        SA = sb("pSA", [128, W])
        SB = sb("pSB", [128, W])
        inv = sb("pinv", [128, NS])
        Db = sb("pDb", [128, NS], BF)
        yb = [sb(f"pyb{i}", [128, 512], BF) for i in range(2)]
        wst = sb("pwst", [128, 2, 128])
        Wp = sb("pWp", [128, 2, 128], BF)
        psc = sb("ppsc", [128, 2])
        self.memset("pool", wst, 0.0)
        for g4 in range(4):
            cc, hf = g4 // 2, g4 % 2
            self.dma(wst[hf * 64:(hf + 1) * 64, cc, hf * 64:(hf + 1) * 64], self.pool_w[L, g4], join=True)
        self.cp("pool", Wp, wst)
        self.load_cols(psc, self.pool_scale[L], 2)
        yield
        yi = 0
        for q in self.seqs:
            n = q["n"]
            pinv = self.c_pinvP if q["prompt"] else self.c_pinvS
            for cc in range(2):
                we = n + 2 * PAD
                self.memset("pool", X[:, 0:PAD], 0.0)
                self.memset("pool", X[:, PAD + n:we], 0.0, join=True)
                self.dma(X[:, PAD:PAD + n], q["PU"][cc * 128:(cc + 1) * 128, :], join=True)
                for hf in range(2):
                    g4 = cc * 2 + hf
                    self.dma(inv[hf * 64:(hf + 1) * 64, 0:n], T(pinv.ap[g4:g4 + 1, :].partition_broadcast(64), pinv.buf), join=(hf > 0))
                yield
                self.tt("dve", SA[:, 1:we - 1], X[:, 0:we - 2], X[:, 1:we - 1], ALU.add)
                yield
                self.tt("dve", SB[:, 2:we - 2], SA[:, 1:we - 3], SA[:, 3:we - 1], ALU.add)
                yield
                if cc == 1:
                    self.tt("dve", SA[:, 4:we - 4], SB[:, 2:we - 6], SB[:, 6:we - 2], ALU.add)
                    yield
                    self.tt("dve", SB[:, 8:we - 8], SA[:, 4:we - 12], SA[:, 12:we - 4], ALU.add)
                    yield
                self.tt("dve", SA[0:64, PAD:PAD + n], SA[0:64, PAD:PAD + n], inv[0:64, 0:n], ALU.mult)
                self.tt("dve", SB[64:128, PAD:PAD + n], SB[64:128, PAD:PAD + n], inv[64:128, 0:n], ALU.mult)
                yield
                self.tt("dve", Db[0:64, 0:n], SA[0:64, PAD:PAD + n], X[0:64, PAD:PAD + n], ALU.subtract)
                self.tt("dve", Db[64:128, 0:n], SB[64:128, PAD:PAD + n], X[64:128, PAD:PAD + n], ALU.subtract, join=True)
                yield
                yield
                bs = 512 if n >= 512 else n
                for bi, b0 in enumerate(range(0, n, bs)):
                    a = self.at
                    p = a["pSs"][a["st"] % len(a["pSs"])]
                    a["st"] += 1
                    self.mm(p[:, 0:bs], Wp[:, cc, :], Db[:, b0:b0 + bs])
                    y = yb[yi % 2]
                    yi += 1
                    self.ts("dve", y[:, 0:bs], p[:, 0:bs], psc[:, cc:cc + 1], ALU.mult)
                    self.dma(q["YT"][512 + cc * 128:512 + (cc + 1) * 128, b0:b0 + bs], y[:, 0:bs], q="pool")
                    if bi % 2 == 1:
                        yield

    def phase6_outproj(self, L):
        with ExitStack() as es:
            sb = lambda n, s, dt=F32: self.sb(es, n, s, dt)
            Wo = sb("Wo", [128, 8, D], BF)
            yT = [sb(f"oyT{i}", [128, 8, 512], BF) for i in range(2)]
            xt = [sb(f"oxt{i}", [128, D]) for i in range(3)]
            tmp = [sb(f"otmp{i}", [128, D]) for i in range(3)]
            xo = [sb(f"oxo{i}", [128, D]) for i in range(3)]
            pO = [self.ps(es, f"opO{i}") for i in range(4)]
            self.load_bf16(Wo, self.WoB, D, D)
            mr = self.mod_rows(es, L, (2,), "p6")
            bi = 0
            ti = 0
            for q in self.seqs:
                n = q["n"]
                g1 = mr[(1 if q["prompt"] else 0, 2)]
                bs = 512 if n >= 512 else n
                for b0 in range(0, n, bs):
                    y = yT[bi % 2]
                    bi += 1
                    ysrc = T(q["YT"].ap[:, b0:b0 + bs].rearrange("(k p) t -> p k t", p=128), q["YT"].buf)
                    self.dma(y[:, 0:4, 0:bs], ysrc[:, 0:4, :])
                    self.dma(y[:, 4:8, 0:bs], ysrc[:, 4:8, :], join=True)
                    for j in range(bs // 128):
                        x, tm, o = xt[ti % 3], tmp[ti % 3], xo[ti % 3]
                        r0 = b0 + j * 128
                        self.dma(x, q["xin_cur"][r0:r0 + 128, :])
                        for hf in range(2):
                            p = pO[(2 * ti + hf) % 4]
                            for k in range(8):
                                self.mm(p, y[:, k, j * 128:(j + 1) * 128], Wo[:, k, hf * 512:(hf + 1) * 512], start=(k == 0), stop=(k == 7))
                            self.tt("dve", tm[:, hf * 512:(hf + 1) * 512], p, g1[:, hf * 512:(hf + 1) * 512], ALU.mult, join=(hf > 0))
                        self.tt("dve", o, tm, x, ALU.add)
                        self.dma(q["XA"][r0:r0 + 128, :], o, q="pool")
                        ti += 1

    def phase7_ffn(self, L):
        last = (L == DEPTH - 1)
        HC = 11
        for half in range(2):
            with ExitStack() as es:
                sb = lambda n, s, dt=F32: self.sb(es, n, s, dt)
                Wg = sb("Wg", [128, 8, HC * 128], BF)
                Wu = sb("Wu", [128, 8, HC * 128], BF)
                Wd = sb("Wd", [128, HC, D], BF)
                xt = [sb(f"fxt{i}", [128, D]) for i in range(4)]
                xr = [sb(f"fxr{i}", [128, D]) for i in range(2)]
                junk = sb("fjunk", [128, D], BF)
                tmp = None
                hn = [sb(f"fhn{i}", [128, D], BF) for i in range(4)]
                ss = sb("fss", [128, 1])
                sd = sb("fsd", [128, 1])
                ss4 = sb("fss4", [128, 4])
                sd4 = sb("fsd4", [128, 4])
                self.eps_col = sb("feps", [128, 1])
                self.memset("dve", self.eps_col, EPS)
                hT = [sb(f"fhT{i}", [128, 8, 512], BF) for i in range(2)]
                aT = sb("faT", [128, HC, 512], BF)
                sg = [sb(f"fsg{i}", [128, 512]) for i in range(2)]
                tm = sb("ftm", [128, D])
                xo = [sb(f"fxo{i}", [128, D]) for i in range(2)]
                yo = [sb(f"fyo{i}", [128, D]) for i in range(2)]
                pT = [self.ps(es, f"fpT{i}") for i in range(2)]
                pG = [self.ps(es, f"fpG{i}") for i in range(2)]
                pU = [self.ps(es, f"fpU{i}") for i in range(2)]
                pD = [self.ps(es, f"fpD{i}") for i in range(2)]
                pTb = [p.bitcast(BF) for p in pT]
                h0c = half * HC * 128
                self.load_bf16(Wg, self.WguB[:, h0c:h0c + HC * 128], D, HC * 128)
                self.load_bf16(Wu, self.WguB[:, FF + h0c:FF + h0c + HC * 128], D, HC * 128)
                self.load_bf16(Wd, self.WdB[h0c:h0c + HC * 128, :], HC * 128, D)
                mr = self.mod_rows(es, L, (3, 4, 5), "p7")
                blocks = []
                for q in self.seqs:
                    bs = 512 if q["n"] >= 512 else q["n"]
                    for b0 in range(0, q["n"], bs):
                        blocks.append((q, b0, bs))
                xi = [0]
                tj = [0]

                def norm_loads(bi):
                    q, b0, bs = blocks[bi]
                    for j in range(bs // 128):
                        self.dma(xt[j], q["XA"][b0 + j * 128:b0 + (j + 1) * 128, :])

                def norm_chain(bi):
                    q, b0, bs = blocks[bi]
                    ci = 1 if q["prompt"] else 0
                    nj_ = bs // 128
                    self.norm_batch(xt[0:nj_], mr[(ci, 4)], mr[(ci, 3)], hn[0:nj_], ss4, sd4)

                def transposes(bi):
                    q, b0, bs = blocks[bi]
                    h_ = hT[bi % 2]
                    for j in range(bs // 128):
                        pt = pTb[tj[0] % 2]
                        tj[0] += 1
                        for k in range(8):
                            self.tr(pt[:, k * 128:(k + 1) * 128], hn[j][:, k * 128:(k + 1) * 128], self.ident_b)
                        self.cp("act", h_[:, :, j * 128:(j + 1) * 128], pt.re("p (k t) -> p k t", k=8), join=(j > 0))

                norm_loads(0)
                norm_chain(0)
                transposes(0)
                ti = 0
                for bi, (q, b0, bs) in enumerate(blocks):
                    ci = 1 if q["prompt"] else 0
                    xres = q["XA"] if half == 0 else q["XB"]
                    h_ = hT[bi % 2]
                    nj = bs // 128
                    if bi + 1 < len(blocks):
                        norm_loads(bi + 1)
                    for c in range(HC):
                        if c == 3 and bi + 1 < len(blocks):
                            norm_chain(bi + 1)
                        g, u = pG[c % 2], pU[c % 2]
                        for k in range(8):
                            self.mm(g[:, 0:bs], Wg[:, k, c * 128:(c + 1) * 128], h_[:, k, 0:bs], start=(k == 0), stop=(k == 7))
                        for k in range(8):
                            self.mm(u[:, 0:bs], Wu[:, k, c * 128:(c + 1) * 128], h_[:, k, 0:bs], start=(k == 0), stop=(k == 7))
                        s_ = sg[c % 2]
                        self.act(s_[:, 0:bs], g[:, 0:bs], AF.Silu)
                        self.tt("dve", aT[:, c, 0:bs], s_[:, 0:bs], u[:, 0:bs], ALU.mult, join=(c > 0))
                    if bi + 1 < len(blocks):
                        transposes(bi + 1)
                    for j in range(nj):
                        r0 = b0 + j * 128
                        x = xr[ti % 2]
                        self.dma(x, xres[r0:r0 + 128, :])
                        o = xo[ti % 2]
                        for hf in range(2):
                            p = pD[hf]
                            for c in range(HC):
                                self.mm(p, aT[:, c, j * 128:(j + 1) * 128], Wd[:, c, hf * 512:(hf + 1) * 512], start=(c == 0), stop=(c == HC - 1))
                            self.tt("dve", tm[:, hf * 512:(hf + 1) * 512], p, mr[(ci, 5)][:, hf * 512:(hf + 1) * 512], ALU.mult, join=(hf > 0))
                        self.tt("dve", o, tm, x, ALU.add)
                        if half == 1 and last:
                            y = yo[ti % 2]
                            self.norm_rows(es, o, self.fin_g, None, y, tmp, junk, ss, sd)
                            self.dma(q["yout"][r0:r0 + 128, :], y, q="pool")
                        else:
                            self.dma((q["XB"] if half == 0 else q["XC"])[r0:r0 + 128, :], o, q="pool")
                        ti += 1
            self.S.barrier()
        if not last:
            for q in self.seqs:
                q["xin_cur"] = q["XC"]


def _build(debug_stop=None, dbg=False, nlayers=DEPTH):
    needed = None
    for _pass in range(2):
        mk = MK(debug_stop, dbg=dbg, needed=needed)
        mk.nlayers = nlayers
        for q in mk.seqs:
            q["xin_cur"] = q["xin"]
        mk.build()
        needed = mk.S.rec
    return mk


def _consts():
    rows = NS // GRID_W
    row = np.repeat(np.arange(rows), GRID_W).astype(np.float32)
    col = np.tile(np.arange(GRID_W), rows).astype(np.float32)
    inv = (10000.0 ** (-np.arange(8, dtype=np.float32) / 8)).astype(np.float32)
    ang = np.concatenate([row[:, None] * inv, col[:, None] * inv], axis=-1)
    cos = np.cos(ang).astype(np.float32).T
    sin = np.sin(ang).astype(np.float32).T
    idx = np.arange(128) % 16
    out = {"c_ident": np.eye(128, dtype=np.float32), "c_ropeC": np.ascontiguousarray(cos[idx]), "c_ropeS": np.ascontiguousarray(sin[idx])}
    for name, n in (("c_pinvS", NS), ("c_pinvP", NP)):
        t = np.arange(n)
        a = np.zeros((4, n), np.float32)
        for g, w in enumerate((2, 4, 8, 16)):
            lo = np.clip(t - w // 2, 0, n)
            hi = np.clip(t + w // 2, 0, n)
            a[g] = 1.0 / (hi - lo).astype(np.float32)
        out[name] = a
    return out


_WNAMES = ["w_ada", "b_ada", "norm1_g", "norm2_g", "w_in", "mla_q_norm_g", "mla_w_uq", "mla_kv_norm_g", "mla_w_ukv",
           "lru_conv_w", "lru_conv_b", "lru_w_r", "lru_b_r", "lru_w_i", "lru_b_i", "lru_lambda", "pool_w", "pool_scale",
           "diff_norm_g", "w_out", "w_gu", "w_down"]


def _in_maps(inp):
    c = lambda a: np.ascontiguousarray(np.asarray(a, dtype=np.float32))
    shared = {k: c(inp[k]) for k in _WNAMES}
    shared["diff_lambda"] = c(np.asarray(inp["diff_lambda"]).reshape(DEPTH, 128))
    shared["final_norm_g"] = c(np.asarray(inp["final_norm_g"]).reshape(1, D))
    shared.update(_consts())
    maps = []
    for i in range(8):
        m = dict(shared)
        m["xs"] = c(inp["x_sample"][i])
        m["xp"] = c(np.asarray(inp["x_prompt"][2 * i:2 * i + 2]).reshape(2 * NP, D))
        m["ckv"] = c(inp["cache_mla_ckv"][i])
        m["ckr"] = c(inp["cache_mla_krope"][i])
        m["cdk"] = c(np.asarray(inp["cache_diff_k"][i]).reshape(DEPTH, PAST, 256))
        m["cdv"] = c(np.asarray(inp["cache_diff_v"][i]).reshape(DEPTH, PAST, 256))
        m["cst"] = c(inp["state_lru"][i])
        m["cb"] = c(np.asarray(inp["c"][i]).reshape(1, D))
        m["cctx"] = c(np.asarray(inp["c_ctx"]).reshape(1, D))
        maps.append(m)
    return maps


def kernel(**inp):
    mk = _build()
    res = run_bass_kernel_spmd(mk.nc, _in_maps(inp), core_ids=list(range(8)))
    R = res.results
    f = lambda k: [np.asarray(r[k], dtype=np.float32) for r in R]
    y_sample = np.stack(f("ys"), 0)
    y_prompt = np.concatenate(f("yp"), 0).reshape(16, NP, D)
    ckv = np.concatenate(f("o_ckv"), 0)
    kr = np.concatenate(f("o_kr"), 0)
    dk = np.concatenate(f("o_dk"), 0).reshape(16, DEPTH, NP, 4, 2, 32)
    dv = np.concatenate(f("o_dv"), 0).reshape(16, DEPTH, NP, 4, 64)
    st = np.concatenate(f("o_st"), 0)
    return (y_prompt, y_sample, ckv, kr, dk, dv, st)
```
